# Optimizing a Trainium2 kernel written in Bass

```python
import math
import jax, jax.numpy as jnp
from jax import lax
import numpy as np

D_MODEL = 2048
BATCH = 8
SEQ = 4096
DEPTH = 4

CHUNK = 64
N_MIXERS = 3
HEAD_DIM = 128
MIX_WIDTH = (3 * D_MODEL) // 4
MIX_HEADS = MIX_WIDTH // HEAD_DIM
MEM_WIDTH = D_MODEL // 4
MEM_HEADS = MEM_WIDTH // HEAD_DIM
BRANCH_WIDTH = MIX_WIDTH + MEM_WIDTH
N_MEM = 256
GMLP_GROUP = 128
Q_BLOCK = 128
DIFF_HEAD_DIM = HEAD_DIM // 2
REL_BUCKETS = 32
REL_MAX_DIST = 128
EPS = 1e-6
NEG = -1e30
A_IN = 2 * MIX_WIDTH + MEM_WIDTH + BRANCH_WIDTH
B_IN = 3 * MIX_WIDTH + MIX_HEADS + MEM_WIDTH + BRANCH_WIDTH
C_IN = 3 * MIX_WIDTH + MEM_WIDTH + BRANCH_WIDTH
N_A = (DEPTH + 2) // 3
N_B = (DEPTH + 1) // 3
N_C = DEPTH // 3

kernel_name = 'hybrid_gmlp_fox_diffattn_memory_trunk'


def rms_norm(x, g):
    xf = x.astype(jnp.float32)
    y = xf * lax.rsqrt(jnp.mean(xf * xf, axis=-1, keepdims=True) + EPS)
    return (y * g.astype(jnp.float32)).astype(x.dtype)


def layer_norm(x, g, b):
    xf = x.astype(jnp.float32)
    mu = jnp.mean(xf, axis=-1, keepdims=True)
    var = jnp.mean(jnp.square(xf - mu), axis=-1, keepdims=True)
    y = (xf - mu) * lax.rsqrt(var + EPS) * g.astype(jnp.float32) + b.astype(jnp.float32)
    return y.astype(x.dtype)


def t5_bucket(rel):
    nb = REL_BUCKETS // 2
    max_exact = nb // 2
    ret = jnp.where(rel > 0, nb, 0)
    n = jnp.abs(rel)
    nf = jnp.maximum(n, 1).astype(jnp.float32)
    large = max_exact + (jnp.log(nf / max_exact) / math.log(REL_MAX_DIST / max_exact)
                         * (nb - max_exact)).astype(jnp.int32)
    large = jnp.minimum(large, nb - 1)
    return ret + jnp.where(n < max_exact, n, large)


def gmlp_spatial_gating(z, ln_g, ln_b, w_s, b_s):
    u, v = jnp.split(z, 2, axis=-1)
    v = layer_norm(v, ln_g, ln_b)
    B, S, _ = v.shape
    v = v.reshape(B, S // GMLP_GROUP, GMLP_GROUP, MIX_HEADS, HEAD_DIM)
    chunk_id = jnp.arange(GMLP_GROUP) // CHUNK
    mask = chunk_id[:, None] >= chunk_id[None, :]
    w = jnp.where(mask[None], w_s, jnp.zeros((), w_s.dtype))
    sv = jnp.einsum('gts,bnsgc->bntgc', w, v) + b_s.T[None, None, :, :, None]
    return u * sv.reshape(B, S, MIX_WIDTH)


def forgetting_attention(q, k, v, f_logit):
    B, S, H, D = q.shape
    c = jnp.cumsum(jax.nn.log_sigmoid(f_logit.astype(jnp.float32)), axis=1).transpose(0, 2, 1)
    scale = D ** -0.5
    outs = []
    for blk in range(S // Q_BLOCK):
        q0, q1 = blk * Q_BLOCK, (blk + 1) * Q_BLOCK
        logits = jnp.einsum('bqhd,bkhd->bhqk', q[:, q0:q1], k[:, :q1]).astype(jnp.float32) * scale
        logits = logits + c[:, :, q0:q1, None] - c[:, :, None, :q1]
        qpos = jnp.arange(q0, q1)
        kpos = jnp.arange(q1)
        logits = jnp.where(kpos[None, :] <= qpos[:, None], logits, NEG)
        p = jax.nn.softmax(logits, axis=-1).astype(v.dtype)
        outs.append(jnp.einsum('bhqk,bkhd->bqhd', p, v[:, :q1]))
    return jnp.concatenate(outs, axis=1).reshape(B, S, H * D)


def differential_attention(q, k, v, rel_bias, lam, subln_g, lam_init):
    B, S, H, _, Dd = q.shape
    lamf = lam.astype(jnp.float32)
    lam_val = (jnp.exp(jnp.sum(lamf[0] * lamf[1])) - jnp.exp(jnp.sum(lamf[2] * lamf[3])) + lam_init)
    table = rel_bias.astype(jnp.float32)
    scale = Dd ** -0.5
    outs = []
    for blk in range(S // Q_BLOCK):
        q0, q1 = blk * Q_BLOCK, (blk + 1) * Q_BLOCK
        qpos = jnp.arange(q0, q1)
        kpos = jnp.arange(q1)
        bias = table[t5_bucket(kpos[None, :] - qpos[:, None])].transpose(2, 0, 1)
        logits = jnp.einsum('bqhmd,bkhmd->bhmqk', q[:, q0:q1], k[:, :q1]).astype(jnp.float32) * scale
        logits = logits + bias[None, :, None]
        mask = (kpos[None, :] // CHUNK) <= (qpos[:, None] // CHUNK)
        p = jax.nn.softmax(jnp.where(mask, logits, NEG), axis=-1)
        a = (p[:, :, 0] - lam_val * p[:, :, 1]).astype(v.dtype)
        outs.append(jnp.einsum('bhqk,bkhe->bqhe', a, v[:, :q1]))
    o = jnp.concatenate(outs, axis=1)
    o = rms_norm(o, subln_g) * (1.0 - lam_init)
    return o.reshape(B, S, H * 2 * Dd)


def memory_attention(q, mem_k, mem_v):
    B, S = q.shape[:2]
    logits = jnp.einsum('bshd,bmhd->bhsm', q, mem_k).astype(jnp.float32) * (HEAD_DIM ** -0.5)
    p = jax.nn.softmax(logits, axis=-1).astype(mem_v.dtype)
    return jnp.einsum('bhsm,bmhd->bshd', p, mem_v).reshape(B, S, MEM_WIDTH)


def setup_inputs(seed: int = 0) -> dict:
    key = jax.random.key(seed)
    it = iter(jax.random.split(key, 32))
    nrm = lambda shape, s: s * jax.random.normal(next(it), shape, jnp.float32)
    gain = lambda shape: 1.0 + 0.05 * jax.random.normal(next(it), shape, jnp.float32)
    din = D_MODEL ** -0.5
    return {
        'x': nrm((BATCH, SEQ, D_MODEL), 1.0),
        'mem': nrm((BATCH, N_MEM, D_MODEL), 1.0),
        'mem_norm_g': gain((D_MODEL,)),
        'rel_bias': nrm((REL_BUCKETS, MIX_HEADS), 0.5),
        'norm_g': gain((DEPTH, D_MODEL)),
        'w_mem_kv': nrm((DEPTH, D_MODEL, 2 * MEM_WIDTH), din),
        'mem_q_norm_g': gain((DEPTH, HEAD_DIM)),
        'mem_k_norm_g': gain((DEPTH, HEAD_DIM)),
        'w_out': nrm((DEPTH, BRANCH_WIDTH, D_MODEL), 0.5 * BRANCH_WIDTH ** -0.5),
        'a_w_in': nrm((N_A, D_MODEL, A_IN), din),
        'a_ln_g': gain((N_A, MIX_WIDTH)),
        'a_ln_b': nrm((N_A, MIX_WIDTH), 0.02),
        'a_w_s': nrm((N_A, MIX_HEADS, GMLP_GROUP, GMLP_GROUP), GMLP_GROUP ** -0.5),
        'a_b_s': 1.0 + nrm((N_A, MIX_HEADS, GMLP_GROUP), 0.1),
        'b_w_in': nrm((N_B, D_MODEL, B_IN), din),
        'b_b_f': jax.random.uniform(next(it), (N_B, MIX_HEADS), jnp.float32, 1.0, 4.0),
        'b_q_norm_g': gain((N_B, HEAD_DIM)),
        'b_k_norm_g': gain((N_B, HEAD_DIM)),
        'c_w_in': nrm((N_C, D_MODEL, C_IN), din),
        'c_q_norm_g': gain((N_C, DIFF_HEAD_DIM)),
        'c_k_norm_g': gain((N_C, DIFF_HEAD_DIM)),
        'c_lam': nrm((N_C, 4, DIFF_HEAD_DIM), 0.1),
        'c_subln_g': gain((N_C, 2 * DIFF_HEAD_DIM)),
    }


def reference(x, mem, mem_norm_g, rel_bias, norm_g, w_mem_kv, mem_q_norm_g, mem_k_norm_g, w_out,
              a_w_in, a_ln_g, a_ln_b, a_w_s, a_b_s,
              b_w_in, b_b_f, b_q_norm_g, b_k_norm_g,
              c_w_in, c_q_norm_g, c_k_norm_g, c_lam, c_subln_g):
    B, S, _ = x.shape
    mem_n = rms_norm(mem, mem_norm_g)
    for i in range(DEPTH):
        kind, j = i % N_MIXERS, i // N_MIXERS
        h = rms_norm(x, norm_g[i])
        w_in = (a_w_in, b_w_in, c_w_in)[kind][j]
        z = h @ w_in
        n_mix = z.shape[-1] - MEM_WIDTH - BRANCH_WIDTH
        mix_in = z[..., :n_mix]
        mem_q = z[..., n_mix:n_mix + MEM_WIDTH]
        gate = z[..., n_mix + MEM_WIDTH:]
        if kind == 0:
            mix_out = gmlp_spatial_gating(jax.nn.gelu(mix_in), a_ln_g[j], a_ln_b[j], a_w_s[j], a_b_s[j])
        elif kind == 1:
            q = rms_norm(mix_in[..., :MIX_WIDTH].reshape(B, S, MIX_HEADS, HEAD_DIM), b_q_norm_g[j])
            k = rms_norm(mix_in[..., MIX_WIDTH:2 * MIX_WIDTH].reshape(B, S, MIX_HEADS, HEAD_DIM), b_k_norm_g[j])
            v = mix_in[..., 2 * MIX_WIDTH:3 * MIX_WIDTH].reshape(B, S, MIX_HEADS, HEAD_DIM)
            f_logit = mix_in[..., 3 * MIX_WIDTH:] + b_b_f[j]
            mix_out = forgetting_attention(q, k, v, f_logit)
        else:
            q = rms_norm(mix_in[..., :MIX_WIDTH].reshape(B, S, MIX_HEADS, 2, DIFF_HEAD_DIM), c_q_norm_g[j])
            k = rms_norm(mix_in[..., MIX_WIDTH:2 * MIX_WIDTH].reshape(B, S, MIX_HEADS, 2, DIFF_HEAD_DIM), c_k_norm_g[j])
            v = mix_in[..., 2 * MIX_WIDTH:].reshape(B, S, MIX_HEADS, 2 * DIFF_HEAD_DIM)
            lam_init = 0.8 - 0.6 * math.exp(-0.3 * i)
            mix_out = differential_attention(q, k, v, rel_bias, c_lam[j], c_subln_g[j], lam_init)
        kv = mem_n @ w_mem_kv[i]
        mk = rms_norm(kv[..., :MEM_WIDTH].reshape(B, N_MEM, MEM_HEADS, HEAD_DIM), mem_k_norm_g[i])
        mv = kv[..., MEM_WIDTH:].reshape(B, N_MEM, MEM_HEADS, HEAD_DIM)
        mq = rms_norm(mem_q.reshape(B, S, MEM_HEADS, HEAD_DIM), mem_q_norm_g[i])
        mem_out = memory_attention(mq, mk, mv)
        branch = jnp.concatenate([mix_out, mem_out], axis=-1) * jax.nn.silu(gate)
        x = x + branch @ w_out[i]
    return x
```

```python
import math
from contextlib import ExitStack

import ml_dtypes
import numpy as np

import concourse.bass as bass
import concourse.mybir as mybir
from concourse.bass_utils import run_bass_kernel_spmd

F32 = mybir.dt.float32
BF16 = mybir.dt.bfloat16
AF = mybir.ActivationFunctionType
ALU = mybir.AluOpType

D = 2048
NCH = 16
NMEM = 256
MIXW = 1536
MEMW = 512
EPS = 1e-6
ENGS = ("sp", "pe", "act", "dve", "pool")
SAME_RAW = True


class Buf:
    def __init__(self, multi=False, excl=False):
        self.w = {}
        self.r = {}
        self.multi = multi
        self.excl = excl


class Chan:
    def __init__(self, sem):
        self.sem = sem
        self.total = 0


class Prog:
    def __init__(self, nc, es):
        self.nc = nc
        self.es = es
        self.recs = {e: [] for e in ENGS}
        self.cnt = {e: 0 for e in ENGS}
        self.seen = {e: {} for e in ENGS}
        self.sem = {e: es.enter_context(nc.semaphore("s_" + e)) for e in ENGS}
        self.chans = []
        self.bufs = []
        self.nchan = 0
        self.pre_barrier = None

    def buf(self, multi=False, excl=False):
        b = Buf(multi, excl)
        self.bufs.append(b)
        return b

    def chan(self):
        self.nchan += 1
        c = Chan(self.es.enter_context(self.nc.semaphore("c%d" % self.nchan)))
        self.chans.append(c)
        return c

    def _semh(self, key):
        return key.sem if isinstance(key, Chan) else self.sem[key]

    def _need(self, eng, waits, ev, kind, is_dma=False):
        key, val, src = ev
        if src == eng and not is_dma:
            if eng == "pe" or not SAME_RAW:
                return
        if self.seen[eng].get(key, 0) >= val:
            return
        self.seen[eng][key] = val
        waits.append((key, val))

    def op(self, eng, fn, reads=(), writes=(), chan=None):
        waits = []
        isd = chan is not None
        for b in reads:
            for ev in b.w.values():
                self._need(eng, waits, ev, "raw", isd)
            if b.excl:
                for ev in b.r.values():
                    if ev[2] != eng:
                        self._need(eng, waits, ev, "rar", isd)
        for b in writes:
            if not b.multi:
                for ev in b.w.values():
                    self._need(eng, waits, ev, "waw", isd)
            for ev in b.r.values():
                self._need(eng, waits, ev, "war", isd)
        if chan is None:
            self.cnt[eng] += 1
            ev = (eng, self.cnt[eng], eng)
            inc = (eng, 1)
        else:
            chan.total += 16
            ev = (chan, chan.total, "dma")
            inc = (chan, 16)
        self.recs[eng].append((waits, fn, inc))
        for b in reads:
            b.r[ev[0]] = ev
        for b in writes:
            if b.multi:
                b.w[ev[0]] = ev
            else:
                b.w = {ev[0]: ev}
                b.r = {}

    def barrier(self):
        if self.pre_barrier is not None:
            self.pre_barrier()
        for e in ENGS:
            waits = []
            for e2 in ENGS:
                if e2 != e and self.cnt[e2] > self.seen[e].get(e2, 0):
                    self.seen[e][e2] = self.cnt[e2]
                    waits.append((e2, self.cnt[e2]))
            for c in self.chans:
                if c.total > self.seen[e].get(c, 0):
                    self.seen[e][c] = c.total
                    waits.append((c, c.total))
            if waits:
                self.recs[e].append((waits, None, None))
        for b in self.bufs:
            b.w = {}
            b.r = {}

    def flush(self):
        with self.nc.Block() as blk:
            for eng, deco in (("sp", blk.sync), ("pe", blk.tensor), ("act", blk.scalar),
                              ("dve", blk.vector), ("pool", blk.gpsimd)):
                recs = self.recs[eng]
                if not recs:
                    continue

                def body(e, recs=recs):
                    for waits, fn, inc in recs:
                        for key, val in waits:
                            e.wait_ge(self._semh(key), val)
                        if fn is None:
                            continue
                        ins = fn(e)
                        ins.then_inc(self._semh(inc[0]), inc[1])

                deco(body)
                self.recs[eng] = []


def _t5_bucket(rel):
    nb = 16
    max_exact = 8
    ret = np.where(rel > 0, nb, 0)
    n = np.abs(rel)
    nf = np.maximum(n, 1).astype(np.float32)
    large = max_exact + (np.log(nf / max_exact) / math.log(128 / max_exact) * (nb - max_exact)).astype(np.int32)
    large = np.minimum(large, nb - 1)
    return ret + np.where(n < max_exact, n, large)


CB_ID, CB_ONES, CB_BONES, CB_MF, CB_MD = 0, 128, 256, 384, 512
NCB = 640
CF_ID, CF_MW, CF_ONES = 0, 128, 256
NCF = 384


def make_consts():
    cb = np.zeros((128, NCB), np.float32)
    i = np.arange(128)
    cb[:, CB_ID:CB_ID + 128] = np.eye(128)
    cb[:, CB_ONES:CB_ONES + 128] = 1.0
    cb[:, CB_BONES:CB_BONES + 128] = (i[:, None] // 64 == i[None, :] // 64)
    cb[:, CB_MF:CB_MF + 128] = (i[:, None] <= i[None, :])
    cb[:, CB_MD:CB_MD + 128] = (i[:, None] // 64 <= i[None, :] // 64)
    sel = np.zeros((128, 12 * 128), np.float32)
    for h in range(12):
        for g in range(3):
            sel[32 * g + h, 128 * h:128 * (h + 1)] = 1.0
    cf = np.zeros((128, NCF), np.float32)
    cf[:, CF_ID:CF_ID + 128] = np.eye(128)
    cf[:, CF_MW:CF_MW + 128] = (i[:, None] // 64 >= i[None, :] // 64)
    cf[:, CF_ONES:CF_ONES + 128] = 1.0
    j = np.arange(383)
    bk = _t5_bucket(127 - j)
    oh = np.zeros((32, 384), np.float32)
    for b in range(32):
        oh[b, 0:383] = (bk == b)
    return cb.astype(ml_dtypes.bfloat16), cf, sel.astype(ml_dtypes.bfloat16), oh


class K:
    pass


def build(S, layers, debug_out=None):
    NT = S // 128
    NG = S // 512
    nc = bass.Bass("TRN2", target_bir_lowering=False)
    dt = nc.dram_tensor

    def din(name, shape, dtype=F32):
        return dt(name, list(shape), dtype, kind="ExternalInput").ap()

    x_in = din("x", [S, D])
    mem_in = din("mem", [NMEM, D])
    mem_norm_g = din("mem_norm_g", [D])
    rel_bias = din("rel_bias", [32, 12])
    norm_g = din("norm_g", [4, D])
    w_mem_kv = din("w_mem_kv", [4, D, 1024])
    mem_q_norm_g = din("mem_q_norm_g", [4, 128])
    mem_k_norm_g = din("mem_k_norm_g", [4, 128])
    w_out = din("w_out", [4, D, D])
    a_w_in = din("a_w_in", [2, D, 5632])
    a_ln_g = din("a_ln_g", [2, MIXW])
    a_ln_b = din("a_ln_b", [2, MIXW])
    a_w_s = din("a_w_s", [2, 12, 128, 128])
    a_b_s = din("a_b_s", [2, 12, 128])
    b_w_in = din("b_w_in", [1, D, 7180])
    b_b_f = din("b_b_f", [1, 12])
    b_q_norm_g = din("b_q_norm_g", [1, 128])
    b_k_norm_g = din("b_k_norm_g", [1, 128])
    c_w_in = din("c_w_in", [1, D, 7168])
    c_q_norm_g = din("c_q_norm_g", [1, 64])
    c_k_norm_g = din("c_k_norm_g", [1, 64])
    c_lam = din("c_lam", [1, 4, 64])
    c_subln_g = din("c_subln_g", [1, 128])
    cbf_in = din("cbf", [128, NCB], BF16)
    cf_in = din("cf32", [128, NCF])
    sel_in = din("selc", [128, 12 * 128], BF16)
    oh_in = din("ohc", [32, 384])
    y_out = dt("y", [S, D], F32, kind="ExternalOutput").ap()
    xs1 = dt("xscr1", [S, D], F32, kind="Internal").ap()
    xs2 = dt("xscr2", [S, D], F32, kind="Internal").ap()
    brT = dt("brT", [D, S], BF16, kind="Internal").ap()
    svT = dt("svT", [MIXW, S], BF16, kind="Internal").ap()
    trd = dt("trd", [12, 384], F32, kind="Internal").ap()

    es = ExitStack()
    with es:
        P = Prog(nc, es)

        uid = [0]

        def sb(stack, name, shape, dtype):
            uid[0] += 1
            return stack.enter_context(nc.sbuf_tensor("%s_%d" % (name, uid[0]), list(shape), dtype))

        cb = sb(es, "cb", [128, NCB], BF16)
        cf = sb(es, "cf", [128, NCF], F32)
        b_cb, b_cf, b_memT = P.buf(), P.buf(), P.buf()
        memTd = dt("memTd", [128, NCH * NMEM], BF16, kind="Internal").ap()
        b_memTd = P.buf(multi=True)
        psbig = [es.enter_context(nc.psum_tensor("ps%d" % i, [128, 1024], F32)) for i in range(4)]
        psum = [psbig[i // 2][:, (i % 2) * 512:(i % 2 + 1) * 512] for i in range(8)]
        b_ps = [P.buf(excl=True) for _ in range(8)]
        c_const = P.chan()
        ident = cb[:, CB_ID:CB_ID + 128]
        ones = cb[:, CB_ONES:CB_ONES + 128]
        bones = cb[:, CB_BONES:CB_BONES + 128]
        maskF = cb[:, CB_MF:CB_MF + 128]
        maskD = cb[:, CB_MD:CB_MD + 128]
        ident32 = cf[:, CF_ID:CF_ID + 128]
        maskW = cf[:, CF_MW:CF_MW + 128]
        ones32 = cf[:, CF_ONES:CF_ONES + 128]
        b_brT = P.buf(multi=True)
        b_svT = P.buf(multi=True)
        b_x = {id(t): P.buf(multi=True) for t in (x_in, xs1, xs2, y_out)}
        bx = lambda t: b_x[id(t)]

        P.op("sp", lambda e: e.dma_start(out=cb[:], in_=cbf_in[:, :]), writes=[b_cb], chan=c_const)
        P.op("sp", lambda e: e.dma_start(out=cf[:], in_=cf_in[:, :]), writes=[b_cf], chan=c_const)

        def mm(out, lhsT, rhs, start, stop, reads, writes):
            P.op("pe", lambda e: e.matmul(out, lhsT, rhs, start=start, stop=stop), reads=reads, writes=writes)

        def act(out, in_, func, reads, writes, bias=None, scale=None):
            kw = {}
            if bias is not None:
                kw["bias"] = bias
            if scale is not None:
                kw["scale"] = scale
            P.op("act", lambda e: e.activation(out=out, in_=in_, func=func, **kw), reads=reads, writes=writes)

        def tt(eng, out, in0, in1, op, reads, writes):
            P.op(eng, lambda e: e.tensor_tensor(out=out, in0=in0, in1=in1, op=op), reads=reads, writes=writes)

        def ts(eng, out, in0, s1, s2, op0, op1, reads, writes):
            if op1 is None:
                P.op(eng, lambda e: e.tensor_scalar(out=out, in0=in0, scalar1=s1, scalar2=None, op0=op0),
                     reads=reads, writes=writes)
            else:
                P.op(eng, lambda e: e.tensor_scalar(out=out, in0=in0, scalar1=s1, scalar2=s2, op0=op0, op1=op1),
                     reads=reads, writes=writes)

        def stt(out, in0, scalar, in1, op0, op1, reads, writes):
            P.op("dve", lambda e: e.scalar_tensor_tensor(out=out, in0=in0, scalar=scalar, in1=in1, op0=op0, op1=op1),
                 reads=reads, writes=writes)

        def recip(out, in_, reads, writes):
            P.op("dve", lambda e: e.reciprocal(out=out, in_=in_), reads=reads, writes=writes)

        deferred = []

        def run_deferred():
            while deferred:
                deferred.pop(0)()

        PROJ = [[5, 6, 7]]
        P.pre_barrier = run_deferred

        def copy(eng, out, in_, reads, writes):
            if eng == "act":
                P.op("act", lambda e: e.copy(out=out, in_=in_), reads=reads, writes=writes)
            else:
                P.op(eng, lambda e: e.tensor_copy(out=out, in_=in_), reads=reads, writes=writes)

        def dma(eng, out, in_, reads, writes, chan, **kw):
            P.op(eng, lambda e: e.dma_start(out=out, in_=in_, **kw), reads=reads, writes=writes, chan=chan)

        def memset(eng, ap, val, writes):
            P.op(eng, lambda e: e.memset(ap, val), writes=writes)

        class Rot:
            def __init__(self, stack, name, n, shape, dtype, chans=False):
                self.t = [sb(stack, "%s%d" % (name, i), shape, dtype) for i in range(n)]
                self.b = [P.buf() for _ in range(n)]
                self.c = [P.chan() for _ in range(n)] if chans else [None] * n
                self.i = 0
                self.n = n

            def next(self):
                i = self.i
                self.i = (i + 1) % self.n
                return self.t[i], self.b[i], self.c[i]

        ps_rr = [0]

        def ps_next(cands):
            i = cands[ps_rr[0] % len(cands)]
            ps_rr[0] += 1
            return psum[i], b_ps[i]

        chan_pool = {}

        def get_chans(name, n):
            if name not in chan_pool:
                chan_pool[name] = [P.chan() for _ in range(n)]
            return chan_pool[name]

        class RotC(Rot):
            def __init__(self, stack, name, n, shape, dtype):
                self.t = [sb(stack, "%s%d" % (name, i), shape, dtype) for i in range(n)]
                self.b = [P.buf() for _ in range(n)]
                self.c = get_chans(name, n)
                self.i = 0
                self.n = n

        def p1_norm_transpose(stack_tensors, src, ntok, gsrc_ap, dstT, dst_bufs_for_tile, src_buf):
            xs, gfm, stat = stack_tensors
            gt, gb, gc = gfm
            dma("sp", gt[:], gsrc_ap.rearrange("(c p) -> p c", p=128), [], [gb], gc, allow_slow_non_contiguous=True)
            for t in range(ntok // 128):
                xt, xb, xc = xs.next()
                st, stb, _ = stat.next()
                dma("sp", xt[:], src[t * 128:(t + 1) * 128, :], [src_buf], [xb], xc)
                jt, jb, _ = xs.junk
                P.op("act", lambda e, jt=jt, xt=xt, st=st: e.activation(out=jt[:], in_=xt[:], func=AF.Square,
                                                                         accum_out=st[:, 0:1]),
                     reads=[xb], writes=[jb, stb])
                act(st[:, 1:2], st[:, 0:1], AF.Ln, [stb, b_eps], [stb], bias=EPS_AP[0], scale=1.0 / D)
                act(st[:, 2:3], st[:, 1:2], AF.Exp, [stb], [stb], scale=-0.5)
                ts("dve", xt[:], xt[:], st[:, 2:3], None, ALU.mult, None, [xb, stb], [xb])
                for jb4 in range(NCH // 4):
                    pt, pb = ps_next([4, 5, 6, 7])
                    for c4 in range(4):
                        c = jb4 * 4 + c4
                        P.op("pe", lambda e, pt=pt, c4=c4, c=c, xt=xt: e.transpose(
                            pt[:, c4 * 128:(c4 + 1) * 128], xt[:, c * 128:(c + 1) * 128], ident32),
                            reads=[xb, b_cf], writes=[pb])
                    tt("dve", dstT[:, jb4 * 4:(jb4 + 1) * 4, t * 128:(t + 1) * 128],
                       pt[:, :].rearrange("p (c n) -> p c n", c=4),
                       gt[:, jb4 * 4:(jb4 + 1) * 4].unsqueeze(2).broadcast_to([128, 4, 128]),
                       ALU.mult, [pb, gb], [dst_bufs_for_tile(t)])

        epst = sb(es, "epst", [128, 4], F32)
        b_eps = P.buf()
        memset("dve", epst[:, 0:1], EPS, [b_eps])
        memset("dve", epst[:, 1:2], 0.0, [b_eps])
        memset("dve", epst[:, 2:3], 1.0, [b_eps])
        EPS_AP = [epst[:, 0:1], epst[:, 1:2], epst[:, 2:3]]

        def load_w(rot, wsrc, col0, ncols=128, cols_out=None):
            wt, wb, wc = rot.next()
            o = wt[:, :, 0:ncols] if cols_out is None else cols_out(wt)
            dma("pool", o, wsrc[:, col0:col0 + ncols].rearrange("(c p) n -> p c n", p=128), [], [wb], wc)
            return wt, wb

        def proj_fm(wt, wb, hT, hbufs, tg, ntok=512):
            pt, pb = ps_next(PROJ[0])
            for c in range(NCH):
                mm(pt[:, 0:ntok], wt[:, c, :], hT[:, c, tg * 512:tg * 512 + ntok], c == 0, c == NCH - 1,
                   [wb, hbufs[tg]], [pb])
            return pt, pb

        def rms_fm_epilogue(tmp, pt, pb, gcol, gbuf, out_ap, out_buf, ntok=512, half=False):
            zq, zb, _ = tmp["zq"].next()
            copy("dve", zq[:, 0:ntok], pt[:, 0:ntok], [pb], [zb])
            rms_from_sbuf(tmp, zq, zb, gcol, gbuf, out_ap, out_buf, ntok, half, defer=True)

        def rms_from_sbuf(tmp, zq, zb, gcol, gbuf, out_ap, out_buf, ntok=512, half=False, defer=False):
            sq, sqb, _ = tmp["sq"].next()
            rs, rsb, _ = tmp["rs"].next()
            tt("pool", sq[:, 0:ntok], zq[:, 0:ntok], zq[:, 0:ntok], ALU.mult, [zb], [sqb])

            def part2():
                p2, p2b = ps_next(PROJ[0])
                mm(p2[:, 0:ntok], bones if half else ones, sq[:, 0:ntok], True, True, [sqb, b_cb], [p2b])
                act(rs[:, 0:ntok], p2[:, 0:ntok], AF.Ln, [p2b, b_eps], [rsb], bias=EPS_AP[0],
                    scale=1.0 / (64 if half else 128))
                act(rs[:, 0:ntok], rs[:, 0:ntok], AF.Exp, [rsb], [rsb], scale=-0.5)
                stt(out_ap, zq[:, 0:ntok], gcol, rs[:, 0:ntok], ALU.mult, ALU.mult, [zb, rsb, gbuf], [out_buf])

            if defer:
                run_deferred()
                deferred.append(part2)
            else:
                part2()

        GATE_PRE = [None]

        def attention(T, nqb, qT, qbufs, kfn, vfn, nkt_fn, collo_fn, bias_fn, mask_fn, aux, nm,
                      finish, gate_w=None):
            run_deferred()
            if nm == 1:
                accs = [[(psum[0], b_ps[0]), (psum[1], b_ps[1])], [(psum[5], b_ps[5]), (psum[6], b_ps[6])]]
                sbanks = [[(psum[2], b_ps[2])], [(psum[3], b_ps[3])], [(psum[4], b_ps[4])]]
            else:
                accs = [[(psum[0], b_ps[0]), (psum[1], b_ps[1]), (psum[2], b_ps[2]), (psum[3], b_ps[3])]]
                sbanks = [[(psum[4], b_ps[4]), (psum[5], b_ps[5])], [(psum[6], b_ps[6]), (psum[7], b_ps[7])]]
            LOOK = len(sbanks) - 1
            for qb in range(nqb):
                acc = accs[qb % len(accs)]
                q0 = qb * 512
                pairs = list(range(nkt_fn(qb)))
                npairs = len(pairs)
                sstate = {}

                def emit_s(i):
                    kt = pairs[i]
                    lo = collo_fn(qb, kt)
                    sb_ = sbanks[i % len(sbanks)]
                    for m in range(nm):
                        st_, stb_ = sb_[m]
                        lhsT, psl, kb = kfn(kt, m)
                        mm(st_[:, lo:512], lhsT, qT[psl, q0 + lo:q0 + 512], True, aux is None,
                           kb + [qbufs[qb]], [stb_])
                        if aux is not None:
                            sel_ap, rows, rbuf = aux
                            mm(st_[:, lo:512], sel_ap, rows[:, q0 + lo:q0 + 512], False, True,
                               [b_cb, rbuf], [stb_])
                    sstate[i] = (kt, lo, sb_)

                def emit_pv(i):
                    kt, lo, sb_ = sstate.pop(i)
                    if nm == 2:
                        pT2, pT2b, _ = T["pT2"].next()
                        big = psbig[2 + (i % 2)]
                        act(pT2[:, :, lo:512], big[:, :].rearrange("p (m n) -> p m n", m=2)[:, :, lo:512], AF.Exp,
                            [sb_[0][1], sb_[1][1]], [pT2b])
                    for m in range(nm):
                        st_, stb_ = sb_[m]
                        if nm == 2:
                            pT, pTb = pT2[:, m, :], pT2b
                        else:
                            pT, pTb, _ = T["pT"].next()
                        bias_ap, bias_bufs = bias_fn(kt)
                        if nm == 2:
                            pass
                        elif bias_ap is None:
                            act(pT[:, lo:512], st_[:, lo:512], AF.Exp, [stb_], [pTb])
                        else:
                            act(pT[:, lo:512], st_[:, lo:512], AF.Exp, [stb_] + bias_bufs, [pTb], bias=bias_ap)
                        if nm == 1:
                            for (c0, mk_ap, mk_bufs) in mask_fn(qb, kt, m):
                                tt("pool", pT[:, c0:c0 + 128], pT[:, c0:c0 + 128], mk_ap, ALU.mult,
                                   [pTb] + mk_bufs, [pTb])
                        elif m == 0:
                            for (c0, w_, mk_ap, mk_bufs) in mask_fn(qb, kt, m):
                                tt("dve", pT2[:, :, c0:c0 + w_], pT2[:, :, c0:c0 + w_],
                                   mk_ap.unsqueeze(1).broadcast_to([128, 2, w_]), ALU.mult,
                                   [pT2b] + mk_bufs, [pT2b])
                        vl, vb = vfn(kt)
                        num, numb = acc[m]
                        den, denb = acc[nm + m]
                        mm(num[:, lo:512], vl, pT[:, lo:512], i == 0, i == npairs - 1, vb + [pTb], [numb])
                        mm(den[:, lo:512], ones, pT[:, lo:512], i == 0, i == npairs - 1, [b_cb, pTb], [denb])

                for i in range(min(LOOK, npairs)):
                    emit_s(i)
                for i in range(npairs):
                    if i + LOOK < npairs:
                        emit_s(i + LOOK)
                    emit_pv(i)
                    if i == min(8, npairs - 1):
                        save = PROJ[0]
                        PROJ[0] = [[2, 3, 4][i % 3]] if nm == 1 else [4 + 2 * (i % 2)]
                        run_deferred()
                        if gate_w is not None:
                            PROJ[0] = [7] if nm == 1 else [5 + 2 * (i % 2)]
                            GATE_PRE[0] = gate_proj(T, gate_w, qb)
                        PROJ[0] = save
                finish(qb, acc)

        def gate_proj(T, gw, qb):
            wt, wb = gw
            pt, pb = proj_fm(wt, wb, T["hT"], T["h_b"], qb)
            gt, gtb, _ = T["g"].next()
            e_, eb, _ = T["rs"].next()
            xs_, xsb, _ = T["gx"].next()
            copy("act", xs_[:], pt[:], [pb], [xsb])
            act(e_[:], xs_[:], AF.Exp, [xsb], [eb], scale=-1.0)
            act(e_[:], e_[:], AF.Ln, [eb, b_eps], [eb], bias=EPS_AP[2])
            act(e_[:], e_[:], AF.Exp, [eb], [eb], scale=-1.0)
            tt("pool", gt[:], xs_[:], e_[:], ALU.mult, [xsb, eb], [gtb])
            return gt, gtb

        def store_branch(T, o_ap, o_bufs, gw, cb_idx, qb, pre=None):
            gt, gtb = gate_proj(T, gw, qb) if pre is None else pre
            stg, stgb, stgc = T["stg"].next()
            tt("pool", stg[:], o_ap, gt[:], ALU.mult, o_bufs + [gtb], [stgb])
            dma("sp", brT[cb_idx * 128:(cb_idx + 1) * 128, qb * 512:(qb + 1) * 512], stg[:], [stgb], [b_brT], stgc)

        def finish_simple(T, gw, cb_idx):
            def fin(qb, acc):
                (num, numb), (den, denb) = acc
                pre = GATE_PRE[0]
                rd, rdb, _ = T["rs"].next()
                o, ob, _ = T["zq"].next()
                recip(rd[:], den[:], [denb], [rdb])
                tt("dve", o[:], num[:], rd[:], ALU.mult, [numb, rdb], [ob])
                store_branch(T, o[:], [ob], gw, cb_idx, qb, pre=pre)
            return fin

        def mem_kv(T, L):
            mkT, mv, memT = T["mkT"], T["mv"], T["memT"]
            gk = T["gsm"]
            for hm in range(4):
                wt, wb = load_w(T["w"], w_mem_kv[L], hm * 128)
                pt, pb = ps_next(PROJ[0])
                for c in range(NCH):
                    mm(pt[:, 0:NMEM], wt[:, c, :], memT[:, c, :], c == 0, c == NCH - 1, [wb, b_memT], [pb])
                rms_fm_epilogue(T, pt, pb, gk[:, 0:1], T["gsm_b"], mkT[:, hm, :], T["mkT_b"], ntok=NMEM)
            for hm in range(4):
                wt, wb = load_w(T["w"], w_mem_kv[L], MEMW + hm * 128)
                pt, pb = ps_next(PROJ[0])
                for mt in range(2):
                    for c in range(NCH):
                        mm(pt[:, mt * 128:(mt + 1) * 128], memT[:, c, mt * 128:(mt + 1) * 128], wt[:, c, :],
                           c == 0, c == NCH - 1, [wb, b_memT], [pb])
                copy("act", mv[:, :, hm * 128:(hm + 1) * 128], pt[:, 0:256].rearrange("p (t n) -> p t n", t=2),
                     [pb], [T["mv_b"]])

        def mem_heads(st, T, L, wsrc, memq_off, gate_off, hT, hbufs):
            alloc_attn(st, T)
            T["memT"] = sb(st, "memT", [128, NCH, NMEM], BF16)
            T["mkT"] = sb(st, "mkT", [128, 4, NMEM], BF16)
            T["mkT_b"] = P.buf()
            T["mv"] = sb(st, "mv", [128, 2, 512], BF16)
            T["mv_b"] = P.buf()
            dma("sp", T["memT"][:].rearrange("p c n -> p (c n)"), memTd[:, :], [b_memTd], [b_memT], get_chans("memT", 1)[0])
            mem_kv(T, L)
            for hm in range(4):
                wt, wb = load_w(T["w"], wsrc, memq_off + hm * 128)
                for tg in range(NG):
                    pt, pb = proj_fm(wt, wb, hT, hbufs, tg)
                    rms_fm_epilogue(T, pt, pb, T["gsm"][:, 1:2], T["gsm_b"],
                                    T["qT"][:, tg * 512:(tg + 1) * 512], T["q_b"][tg])
                gw = load_w(T["w"], wsrc, gate_off + MIXW + hm * 128)
                attention(T, NG, T["qT"], T["q_b"],
                          kfn=lambda kt, m, hm=hm: (T["mkT"][:, hm, kt * 128:(kt + 1) * 128], slice(0, 128), [T["mkT_b"]]),
                          vfn=lambda kt, hm=hm: (T["mv"][:, kt, hm * 128:(hm + 1) * 128], [T["mv_b"]]),
                          nkt_fn=lambda qb: 2, collo_fn=lambda qb, kt: 0,
                          bias_fn=lambda kt: (None, []), mask_fn=lambda qb, kt, m: [],
                          aux=None, nm=1, finish=finish_simple(T, gw, 12 + hm), gate_w=gw)

        def load_small(T, L, kind, j):
            g = T["gsm"]
            gb = T["gsm_b"]
            gc = T["gsm_c"]
            tmp = T["gsm_raw"]
            dma("sp", tmp[:, 0:1], mem_k_norm_g[L].unsqueeze(1), [], [gb], gc)
            dma("sp", tmp[:, 1:2], mem_q_norm_g[L].unsqueeze(1), [], [gb], gc)
            if kind == 1:
                dma("sp", tmp[:, 2:3], b_q_norm_g[j].unsqueeze(1), [], [gb], gc)
                dma("sp", tmp[:, 3:4], b_k_norm_g[j].unsqueeze(1), [], [gb], gc)
            if kind == 2:
                for hh in range(2):
                    dma("sp", tmp[hh * 64:(hh + 1) * 64, 2:3], c_q_norm_g[j].unsqueeze(1), [], [gb], gc)
                    dma("sp", tmp[hh * 64:(hh + 1) * 64, 3:4], c_k_norm_g[j].unsqueeze(1), [], [gb], gc)
                dma("sp", tmp[:, 4:5], c_subln_g[j].unsqueeze(1), [], [gb], gc)
            copy("dve", g[:, 0:1], tmp[:, 0:1], [gb], [gb])
            ts("dve", g[:, 1:2], tmp[:, 1:2], 128 ** -0.5, None, ALU.mult, None, [gb], [gb])
            if kind == 1:
                ts("dve", g[:, 2:3], tmp[:, 2:3], 128 ** -0.5, None, ALU.mult, None, [gb], [gb])
                copy("dve", g[:, 3:4], tmp[:, 3:4], [gb], [gb])
            if kind == 2:
                ts("dve", g[:, 2:3], tmp[:, 2:3], 64 ** -0.5, None, ALU.mult, None, [gb], [gb])
                copy("dve", g[:, 3:4], tmp[:, 3:4], [gb], [gb])
                lam_init = 0.8 - 0.6 * math.exp(-0.3 * L)
                ts("dve", g[:, 4:5], tmp[:, 4:5], 1.0 - lam_init, None, ALU.mult, None, [gb], [gb])

        def alloc_attn(stk, T, nm=1):
            T["qT"] = sb(stk, "qT", [128, S], BF16)
            T["q_b"] = [P.buf() for _ in range(NG)]
            if nm == 1:
                T["pT"] = Rot(stk, "pT", 4, [128, 512], BF16)
            else:
                T["pT2"] = Rot(stk, "pT2", 3, [128, 2, 512], BF16)
                T["on"] = Rot(stk, "on", 2, [128, 512], F32)

        def common_tensors(st, kind):
            T = {}
            T["hT"] = sb(st, "hT", [128, NCH, S], BF16)
            T["h_b"] = [P.buf() for _ in range(NG)]
            T["w"] = RotC(st, "w", 3, [128, NCH, 128], BF16)
            T["g"] = Rot(st, "g", 3, [128, 512], BF16)
            T["gx"] = Rot(st, "gx", 2, [128, 512], BF16)
            T["zq"] = Rot(st, "zq", 3 if kind == 0 else 4, [128, 512], F32)
            T["sq"] = Rot(st, "sq", 2 if kind == 0 else 3, [128, 512], BF16)
            T["rs"] = Rot(st, "rs", 3, [128, 512], F32)
            T["stg"] = RotC(st, "stg", 2, [128, 512], BF16)
            T["gsm"] = sb(st, "gsm", [128, 8], F32)
            T["gsm_raw"] = sb(st, "gsmr", [128, 8], F32)
            T["gsm_b"] = P.buf()
            T["gsm_c"] = get_chans("gsm", 1)[0]
            T["gfm"] = (sb(st, "gfm", [128, NCH], F32), P.buf(), get_chans("gfm", 1)[0])
            return T

        def p1_tensors(st):
            xs = RotC(st, "xs", 2, [128, D], F32)
            xs.junk = (sb(st, "xjunk", [128, D], BF16), P.buf(), None)
            stat = Rot(st, "stat", 2, [128, 4], F32)
            return xs, stat

        with ExitStack() as st:
            xs, stat = p1_tensors(st)
            gfm = (sb(st, "gfm0", [128, NCH], F32), P.buf(), get_chans("gfm", 1)[0])
            b_memsrc = P.buf()
            memT0 = sb(st, "memT0", [128, NCH, NMEM], BF16)
            p1_norm_transpose((xs, gfm, stat), mem_in, NMEM, mem_norm_g, memT0, lambda t: b_memT, b_memsrc)
            dma("sp", memTd[:, :], memT0[:].rearrange("p c n -> p (c n)"), [b_memT], [b_memTd], get_chans("memT", 1)[0])
            P.barrier()
            P.flush()

        def proj_v(T, wt, wb, vt, vbufs):
            for tb in range(NG):
                pt, pb = ps_next(PROJ[0])
                for t4 in range(4):
                    t = tb * 4 + t4
                    for c in range(NCH):
                        mm(pt[:, t4 * 128:(t4 + 1) * 128], T["hT"][:, c, t * 128:(t + 1) * 128], wt[:, c, :],
                           c == 0, c == NCH - 1, [wb, T["h_b"][tb]], [pb])
                copy("act", vt[:, tb * 4:(tb + 1) * 4, :], pt[:].rearrange("p (t n) -> p t n", t=4), [pb], [vbufs[tb]])

        def layer_b(st, T, L, j, wsrc, gate_off):
            alloc_attn(st, T)
            kT = sb(st, "kT", [128, S], BF16)
            k_b = [P.buf() for _ in range(NG)]
            vt = sb(st, "vt", [128, NT, 128], BF16)
            v_b = [P.buf() for _ in range(NG)]
            rows = sb(st, "rows", [128, S], BF16)
            rows_b = P.buf()
            selt = sb(st, "selt", [128, 12 * 128], BF16)
            dma("sp", selt[:], sel_in[:, :], [], [b_cb], get_chans("selt", 1)[0])
            negc = sb(st, "negc", [128, NT, 12], F32)
            negc_b = P.buf()
            sm = sb(st, "fsm", [128, 4], F32)
            sm_b = P.buf()
            smc = get_chans("fsm", 1)[0]
            memset("dve", sm[:], 0.0, [sm_b])
            memset("dve", sm[:, 2:3], 1.0, [sm_b])
            for g in range(3):
                dma("sp", sm[32 * g:32 * g + 12, 0:1], b_b_f[j].unsqueeze(1), [], [sm_b], smc)
            wt, wb, wc = T["w"].next()
            P.op("pool", lambda e: e.memset(wt[:], 0.0), writes=[wb])
            for g in range(3):
                dma("pool", wt[:, :, 32 * g:32 * g + 12],
                    wsrc[:, 3 * MIXW:3 * MIXW + 12].rearrange("(c p) n -> p c n", p=128), [], [wb], wc)
            for tg in range(NG):
                pt, pb = proj_fm(wt, wb, T["hT"], T["h_b"], tg)
                ls, lsb, _ = T["zq"].next()
                cc, ccb, _ = T["rs"].next()
                act(ls[:], pt[:], AF.Sigmoid, [pb, sm_b], [lsb], bias=sm[:, 0:1])
                act(ls[:], ls[:], AF.Ln, [lsb], [lsb])
                init = 0.0 if tg == 0 else sm[:, 1:2]
                P.op("dve", lambda e, cc=cc, ls=ls, init=init: e.tensor_tensor_scan(
                    out=cc[:], data0=sm[:, 2:3].broadcast_to([128, 512]), data1=ls[:], initial=init,
                    op0=ALU.mult, op1=ALU.add), reads=[lsb, sm_b], writes=[ccb])
                copy("dve", sm[:, 1:2], cc[:, 511:512], [ccb], [sm_b])
                sl = slice(tg * 512, (tg + 1) * 512)
                copy("act", rows[:, sl], cc[:], [ccb], [rows_b])
                r1, r1b, _ = T["zq"].next()
                tt("dve", r1[:], cc[:], rows[:, sl], ALU.subtract, [ccb, rows_b], [r1b])
                md, mdb, _ = T["sq"].next()
                copy("act", md[:], r1[:], [r1b], [mdb])
                copy("pool", rows[32:64, sl], md[32:64, :], [mdb], [rows_b])
                tt("dve", r1[:], r1[:], md[:], ALU.subtract, [r1b, mdb], [r1b])
                copy("act", rows[64:96, sl], r1[64:96, :], [r1b], [rows_b])
                p2, p2b = ps_next(PROJ[0])
                for i4 in range(4):
                    P.op("pe", lambda e, p2=p2, cc=cc, i4=i4: e.transpose(
                        p2[:, i4 * 32:(i4 + 1) * 32], cc[0:32, i4 * 128:(i4 + 1) * 128], ident32[0:32, 0:32]),
                        reads=[ccb, b_cf], writes=[p2b])
                ts("dve", negc[:, tg * 4:(tg + 1) * 4, :],
                   p2[:, 0:128].rearrange("p (t n) -> p t n", t=4)[:, :, 0:12], -1.0, None, ALU.mult, None,
                   [p2b], [negc_b])
            for h in range(12):
                wq = load_w(T["w"], wsrc, h * 128)
                for tg in range(NG):
                    pt, pb = proj_fm(wq[0], wq[1], T["hT"], T["h_b"], tg)
                    rms_fm_epilogue(T, pt, pb, T["gsm"][:, 2:3], T["gsm_b"],
                                    T["qT"][:, tg * 512:(tg + 1) * 512], T["q_b"][tg])
                wk = load_w(T["w"], wsrc, MIXW + h * 128)
                for tg in range(NG):
                    pt, pb = proj_fm(wk[0], wk[1], T["hT"], T["h_b"], tg)
                    rms_fm_epilogue(T, pt, pb, T["gsm"][:, 3:4], T["gsm_b"],
                                    kT[:, tg * 512:(tg + 1) * 512], k_b[tg])
                wv = load_w(T["w"], wsrc, 2 * MIXW + h * 128)
                proj_v(T, wv[0], wv[1], vt, v_b)
                gw = load_w(T["w"], wsrc, gate_off + h * 128)
                sel_h = selt[:, 128 * h:128 * (h + 1)]
                attention(T, NG, T["qT"], T["q_b"],
                          kfn=lambda kt, m: (kT[:, kt * 128:(kt + 1) * 128], slice(0, 128), [k_b[kt // 4]]),
                          vfn=lambda kt: (vt[:, kt, :], [v_b[kt // 4]]),
                          nkt_fn=lambda qb: 4 * qb + 4,
                          collo_fn=lambda qb, kt: max(0, kt - 4 * qb) * 128,
                          bias_fn=lambda kt, h=h: (negc[:, kt, h:h + 1], [negc_b]),
                          mask_fn=lambda qb, kt, m: ([((kt - 4 * qb) * 128, maskF, [b_cb])] if kt >= 4 * qb else []),
                          aux=(sel_h, rows, rows_b), nm=1, finish=finish_simple(T, gw, h), gate_w=gw)

        def layer_a(st, T, L, j, wsrc, gate_off):
            hT, hbufs = T["hT"], T["h_b"]
            addT = sb(st, "addT", [128, 12, 128], F32)
            wmT = sb(st, "wmT", [128, 12, 128], BF16)
            gb12 = sb(st, "gb12", [128, 24], F32)
            bc_b, wm_b = P.buf(), P.buf()
            bcc = get_chans("abc", 1)[0]
            dma("sp", gb12[:, 0:12], a_ln_g[j].rearrange("(g c) -> c g", c=128), [], [bc_b], bcc,
                allow_slow_non_contiguous=True)
            dma("sp", gb12[:, 12:24], a_ln_b[j].rearrange("(g c) -> c g", c=128), [], [bc_b], bcc,
                allow_slow_non_contiguous=True)
            with ExitStack() as st_t:
                bsbc = sb(st_t, "bsbc", [128, 12, 128], F32)
                wmf = Rot(st_t, "wmf", 2, [128, 128], F32)
                wl = RotC(st_t, "wsl", 2, [128, 128], F32)
                bs_b = P.buf()
                dma("sp", bsbc[:].rearrange("p g t -> p (g t)"),
                    a_b_s[j].rearrange("g t -> (g t)").partition_broadcast(128), [], [bs_b], get_chans("abs", 1)[0])
                for g in range(12):
                    wt_, wtb, wtc = wl.next()
                    dma("sp", wt_[:], a_w_s[j, g], [], [wtb], wtc)
                    tt("dve", wt_[:], wt_[:], maskW, ALU.mult, [wtb, b_cf], [wtb])
                    pt, pb = ps_next(PROJ[0])
                    P.op("pe", lambda e, pt=pt, wt_=wt_: e.transpose(pt[:, 0:128], wt_[:], ident32),
                         reads=[wtb, b_cf], writes=[pb])
                    copy("act", wmT[:, g, :], pt[:, 0:128], [pb], [wm_b])
                    p2, p2b = ps_next(PROJ[0])
                    mm(p2[:, 0:128], ones, wmT[:, g, :], True, True, [wm_b, b_cb], [p2b])
                    stt(addT[:, g, :], p2[:, 0:128], gb12[:, 12 + g:13 + g], bsbc[:, g, :], ALU.mult, ALU.add,
                        [p2b, bc_b, bs_b], [wm_b])
                P.barrier()
                P.flush()
            gv = [sb(st, "gv", [128, 4, MIXW], BF16) for _ in range(2)]
            gv_b = [[P.buf() for _ in range(4)] for _ in range(2)]
            vh = Rot(st, "vh", 2, [128, MIXW], BF16)
            lnst = Rot(st, "lnst", 2, [128, 24], F32)
            svs = RotC(st, "svs", 2, [128, 4, 128], BF16)
            svl = RotC(st, "svl", 2, [128, 512], BF16)

            def v_group(gi):
                gvt, gvb = gv[gi % 2], gv_b[gi % 2]
                for vb in range(12):
                    wt, wb = load_w(T["w"], wsrc, MIXW + vb * 128)
                    pt, pb = ps_next(PROJ[0])
                    for t4 in range(4):
                        t = gi * 4 + t4
                        for c in range(NCH):
                            mm(pt[:, t4 * 128:(t4 + 1) * 128], hT[:, c, t * 128:(t + 1) * 128], wt[:, c, :],
                               c == 0, c == NCH - 1, [wb, hbufs[gi]], [pb])
                    act(gvt[:, :, vb * 128:(vb + 1) * 128], pt[:].rearrange("p (t n) -> p t n", t=4),
                        AF.Gelu_apprx_tanh, [pb], gvb)

            def ln_group(gi):
                gvt, gvb = gv[gi % 2], gv_b[gi % 2]
                for t4 in range(4):
                    t = gi * 4 + t4
                    ls_, lsb, _ = lnst.next()
                    for k3 in range(3):
                        P.op("dve", lambda e, ls_=ls_, k3=k3, t4=t4, gvt=gvt: e.bn_stats(
                            out=ls_[:, k3 * 6:(k3 + 1) * 6], in_=gvt[:, t4, k3 * 512:(k3 + 1) * 512]),
                            reads=[gvb[t4]], writes=[lsb])
                    P.op("dve", lambda e, ls_=ls_: e.bn_aggr(out=ls_[:, 18:20], in_=ls_[:, 0:18]),
                         reads=[lsb], writes=[lsb])
                    act(ls_[:, 20:21], ls_[:, 19:20], AF.Ln, [lsb, b_eps], [lsb], bias=EPS_AP[0])
                    act(ls_[:, 21:22], ls_[:, 20:21], AF.Exp, [lsb], [lsb], scale=-0.5)
                    vt_, vtb, _ = vh.next()
                    ts("dve", vt_[:], gvt[:, t4, :], ls_[:, 18:19], ls_[:, 21:22], ALU.subtract, ALU.mult,
                       [gvb[t4], lsb], [vtb])
                    for g4 in range(3):
                        pt, pb = ps_next(PROJ[0])
                        for gi4 in range(4):
                            g = g4 * 4 + gi4
                            mm(pt[:, gi4 * 128:(gi4 + 1) * 128], vt_[:, g * 128:(g + 1) * 128], wmT[:, g, :],
                               True, True, [vtb, wm_b], [pb])
                        sv_, svb, svc = svs.next()
                        for gi4 in range(4):
                            g = g4 * 4 + gi4
                            stt(sv_[:, gi4, :], pt[:, gi4 * 128:(gi4 + 1) * 128], gb12[:, g:g + 1], addT[:, g, :],
                                ALU.mult, ALU.add, [pb, bc_b, wm_b], [svb])
                        dma("sp", svT[g4 * 512:(g4 + 1) * 512, t * 128:(t + 1) * 128].rearrange("(g c) n -> c g n", c=128),
                            sv_[:], [svb], [b_svT], svc)

            v_group(0)
            for gi in range(1, NG):
                v_group(gi)
                ln_group(gi - 1)
            ln_group(NG - 1)
            for g in range(12):
                wu = load_w(T["w"], wsrc, g * 128)
                gw = load_w(T["w"], wsrc, gate_off + g * 128)
                for tg in range(NG):
                    sl_, slb, slc = svl.next()
                    dma("sp", sl_[:], svT[g * 128:(g + 1) * 128, tg * 512:(tg + 1) * 512], [b_svT], [slb], slc)
                    pt, pb = proj_fm(wu[0], wu[1], hT, hbufs, tg)
                    u_, ub, _ = T["zq"].next()
                    act(u_[:], pt[:], AF.Gelu_apprx_tanh, [pb], [ub])
                    tt("pool", u_[:], u_[:], sl_[:], ALU.mult, [ub, slb], [ub])
                    store_branch(T, u_[:], [ub], gw, g, tg)

        def layer_c(st, T, L, j, wsrc, gate_off):
            Mall = sb(st, "Mall", [128, 12, 256], BF16)
            M_b = P.buf()
            csm = sb(st, "csm", [128, 40], F32)
            csm_b = P.buf()
            cc_ = get_chans("csm", 1)[0]
            lam_init = 0.8 - 0.6 * math.exp(-0.3 * L)
            dma("sp", csm[:, 0:12], rel_bias[15].partition_broadcast(128), [], [csm_b], cc_)
            ts("dve", csm[:, 12:24], csm[:, 0:12], -1.0, None, ALU.mult, None, [csm_b], [csm_b])
            with ExitStack() as st_t:
                tb_ = P.buf()
                lamt = sb(st_t, "lamt", [128, 256], F32)
                Ft = sb(st_t, "Ft", [128, 12, 256], F32)
                rb32 = sb(st_t, "rb32", [32, 12], F32)
                oht = sb(st_t, "oht", [32, 384], F32)
                dma("sp", oht[:], oh_in[:, :], [], [tb_], get_chans("oht", 1)[0])
                trs = sb(st_t, "trs", [12, 384], F32)
                tcn = get_chans("ctmp", 1)[0]
                dma("sp", lamt[:], c_lam[j].rearrange("a d -> (a d)").partition_broadcast(128), [], [tb_], tcn)
                tt("dve", lamt[:, 0:64], lamt[:, 0:64], lamt[:, 64:128], ALU.mult, [tb_], [tb_])
                tt("dve", lamt[:, 128:192], lamt[:, 128:192], lamt[:, 192:256], ALU.mult, [tb_], [tb_])
                P.op("dve", lambda e: e.reduce_sum(out=csm[:, 24:25], in_=lamt[:, 0:64], axis=mybir.AxisListType.X),
                     reads=[tb_], writes=[csm_b])
                P.op("dve", lambda e: e.reduce_sum(out=csm[:, 25:26], in_=lamt[:, 128:192], axis=mybir.AxisListType.X),
                     reads=[tb_], writes=[csm_b])
                act(csm[:, 26:28], csm[:, 24:26], AF.Exp, [csm_b], [csm_b])
                tt("dve", csm[:, 28:29], csm[:, 27:28], csm[:, 26:27], ALU.subtract, [csm_b], [csm_b])
                ts("dve", csm[:, 30:31], csm[:, 28:29], -lam_init, None, ALU.add, None, [csm_b], [csm_b])
                dma("sp", rb32[:], rel_bias[:, :], [], [tb_], tcn)
                pt, pb = ps_next(PROJ[0])
                mm(pt[0:12, 0:383], rb32[:], oht[:, 0:383], True, True, [tb_], [pb])
                copy("dve", trs[:, 0:383], pt[0:12, 0:383], [pb], [tb_])
                memset("dve", trs[:, 383:384], 0.0, [tb_])
                b_trd = P.buf()
                dma("sp", trd[:, :], trs[:], [tb_], [b_trd], tcn)
                ft_b = P.buf(multi=True)
                for k in range(128):
                    dma("sp", Ft[k:k + 1, :, :], trd[:, 127 - k:127 - k + 256].unsqueeze(0), [b_trd], [ft_b], tcn)
                for h in range(12):
                    act(Mall[:, h, :], Ft[:, h, :], AF.Exp, [ft_b, csm_b], [M_b], bias=csm[:, 12 + h:13 + h])
                    tt("pool", Mall[:, h, 0:128], Mall[:, h, 0:128], maskD, ALU.mult, [M_b, b_cb], [M_b])
                P.barrier()
                P.flush()
            alloc_attn(st, T, nm=2)
            kT = sb(st, "kT", [128, S], BF16)
            k_b = [P.buf() for _ in range(NG)]
            vt = sb(st, "vt", [128, NT, 128], BF16)
            v_b = [P.buf() for _ in range(NG)]

            def fin_factory(h, gw):
                def fin(qb, acc):
                    (n0, n0b), (n1, n1b), (d0, d0b), (d1, d1b) = acc
                    r0, r0b, _ = T["rs"].next()
                    t0, t0b, _ = T["zq"].next()
                    r1, r1b, _ = T["rs"].next()
                    t1, t1b, _ = T["zq"].next()
                    copy("dve", r0[:], d0[:], [d0b], [r0b])
                    copy("dve", r1[:], d1[:], [d1b], [r1b])
                    copy("dve", t0[:], n0[:], [n0b], [t0b])
                    copy("dve", t1[:], n1[:], [n1b], [t1b])
                    pre = GATE_PRE[0]
                    act(r0[:], r0[:], AF.Ln, [r0b], [r0b])
                    act(r0[:], r0[:], AF.Exp, [r0b], [r0b], scale=-1.0)
                    act(r1[:], r1[:], AF.Ln, [r1b], [r1b])
                    act(r1[:], r1[:], AF.Exp, [r1b], [r1b], scale=-1.0)
                    tt("dve", t0[:], t0[:], r0[:], ALU.mult, [t0b, r0b], [t0b])
                    tt("dve", t1[:], t1[:], r1[:], ALU.mult, [t1b, r1b], [t1b])
                    stt(t0[:], t1[:], csm[:, 30:31], t0[:], ALU.mult, ALU.add, [t1b, t0b, csm_b], [t0b])
                    on, onb, _ = T["on"].next()
                    rms_from_sbuf(T, t0, t0b, T["gsm"][:, 4:5], T["gsm_b"], on[:], onb, defer=True)
                    deferred.append(lambda: store_branch(T, on[:], [onb], gw, h, qb, pre=pre))
                return fin

            def mask_fn_factory(h):
                def mask_fn(qb, kt, m):
                    rel = kt - 4 * qb
                    if rel == -1:
                        return [(0, 128, Mall[:, h, 128:256], [M_b])]
                    if 0 <= rel <= 2:
                        return [(rel * 128, 256, Mall[:, h, 0:256], [M_b])]
                    if rel == 3:
                        return [(384, 128, Mall[:, h, 0:128], [M_b])]
                    return []
                return mask_fn

            for h in range(12):
                wq = load_w(T["w"], wsrc, h * 128)
                for tg in range(NG):
                    pt, pb = proj_fm(wq[0], wq[1], T["hT"], T["h_b"], tg)
                    rms_fm_epilogue(T, pt, pb, T["gsm"][:, 2:3], T["gsm_b"],
                                    T["qT"][:, tg * 512:(tg + 1) * 512], T["q_b"][tg], half=True)
                wk = load_w(T["w"], wsrc, MIXW + h * 128)
                for tg in range(NG):
                    pt, pb = proj_fm(wk[0], wk[1], T["hT"], T["h_b"], tg)
                    rms_fm_epilogue(T, pt, pb, T["gsm"][:, 3:4], T["gsm_b"],
                                    kT[:, tg * 512:(tg + 1) * 512], k_b[tg], half=True)
                wv = load_w(T["w"], wsrc, 2 * MIXW + h * 128)
                proj_v(T, wv[0], wv[1], vt, v_b)
                gw = load_w(T["w"], wsrc, gate_off + h * 128)
                attention(T, NG, T["qT"], T["q_b"],
                          kfn=lambda kt, m: (kT[m * 64:(m + 1) * 64, kt * 128:(kt + 1) * 128],
                                             slice(m * 64, (m + 1) * 64), [k_b[kt // 4]]),
                          vfn=lambda kt: (vt[:, kt, :], [v_b[kt // 4]]),
                          nkt_fn=lambda qb: 4 * qb + 4,
                          collo_fn=lambda qb, kt: max(0, kt - 4 * qb) * 128,
                          bias_fn=lambda kt: (None, []),
                          mask_fn=mask_fn_factory(h),
                          aux=None, nm=2, finish=fin_factory(h, gw), gate_w=gw)

        xsrc = x_in
        chain = [xs1, xs2]
        for li, (L, kind, j) in enumerate(layers):
            xdst = y_out if li == len(layers) - 1 else chain[li % 2]
            with ExitStack() as st:
                T = common_tensors(st, kind)
                hT, hbufs = T["hT"], T["h_b"]
                with ExitStack() as st1:
                    xs, stat = p1_tensors(st1)
                    p1_norm_transpose((xs, T["gfm"], stat), xsrc, S, norm_g[L], hT, lambda t: hbufs[t // 4], bx(xsrc))
                    P.barrier()
                    P.flush()
                load_small(T, L, kind, j)
                st_mix = ExitStack()
                st.enter_context(st_mix)
                PROJ[0] = {0: [0, 1, 2, 3, 4, 5, 6, 7], 1: [7, 2, 3, 4], 2: [4, 5, 6, 7]}.get(kind, [5, 6, 7])
                if kind < 0:
                    wsrc = a_w_in[j]
                    memq_off, gate_off = 2 * MIXW, 2 * MIXW + MEMW
                    for cbi in range(12):
                        for qb in range(NG):
                            stg, stgb, stgc = T["stg"].next()
                            memset("pool", stg[:], 0.0, [stgb])
                            dma("sp", brT[cbi * 128:(cbi + 1) * 128, qb * 512:(qb + 1) * 512], stg[:], [stgb], [b_brT], stgc)
                elif kind == 0:
                    wsrc = a_w_in[j]
                    memq_off, gate_off = 2 * MIXW, 2 * MIXW + MEMW
                    layer_a(st_mix, T, L, j, wsrc, gate_off)
                elif kind == 1:
                    wsrc = b_w_in[j]
                    memq_off, gate_off = 3 * MIXW + 12, 3 * MIXW + 12 + MEMW
                    layer_b(st_mix, T, L, j, wsrc, gate_off)
                else:
                    wsrc = c_w_in[j]
                    memq_off, gate_off = 3 * MIXW, 3 * MIXW + MEMW
                    layer_c(st_mix, T, L, j, wsrc, gate_off)
                P.barrier()
                P.flush()
                st_mix.close()
                with ExitStack() as st2:
                    PROJ[0] = [7, 2, 3, 4]
                    mem_heads(st2, T, L, wsrc, memq_off, gate_off, hT, hbufs)
                    P.barrier()
                    P.flush()
            with ExitStack() as st:
                wo = sb(st, "wo", [128, NCH, D], BF16)
                wo_b = [P.buf() for _ in range(NCH)]
                wo_c = get_chans("wo", NCH)
                br = RotC(st, "br", 2, [128, NCH, 512], BF16)
                xo = RotC(st, "xo", 2, [128, D], F32)
                xn = RotC(st, "xn", 2, [128, D], F32)
                for c in range(NCH):
                    dma("pool", wo[:, c, :], w_out[L][c * 128:(c + 1) * 128, :], [], [wo_b[c]], wo_c[c])
                for tg in range(NG):
                    bt, bb, bc = br.next()
                    dma("sp", bt[:], brT[:, tg * 512:(tg + 1) * 512].rearrange("(c p) n -> p c n", p=128),
                        [b_brT], [bb], bc)
                    for t4 in range(4):
                        t = tg * 4 + t4
                        xt, xb, xc = xo.next()
                        dma("sp", xt[:], xsrc[t * 128:(t + 1) * 128, :], [bx(xsrc)], [xb], xc)
                        banks = [(psum[i], b_ps[i]) for i in ((0, 1, 2, 3) if t % 2 == 0 else (4, 5, 6, 7))]
                        for c in range(NCH):
                            for jd in range(4):
                                pt, pb = banks[jd]
                                mm(pt[:], bt[:, c, t4 * 128:(t4 + 1) * 128], wo[:, c, jd * 512:(jd + 1) * 512],
                                   c == 0, c == NCH - 1, [bb, wo_b[c]], [pb])
                        nt_, nb_, ncn = xn.next()
                        for jd in range(4):
                            pt, pb = banks[jd]
                            tt("dve", nt_[:, jd * 512:(jd + 1) * 512], pt[:], xt[:, jd * 512:(jd + 1) * 512],
                               ALU.add, [pb, xb], [nb_])
                        dma("sp", xdst[t * 128:(t + 1) * 128, :], nt_[:], [nb_], [bx(xdst)], ncn)
                P.barrier()
                P.flush()
            xsrc = xdst
    return nc


_PARAMS = ("mem_norm_g", "rel_bias", "norm_g", "w_mem_kv", "mem_q_norm_g", "mem_k_norm_g", "w_out",
           "a_w_in", "a_ln_g", "a_ln_b", "a_w_s", "a_b_s", "b_w_in", "b_b_f", "b_q_norm_g", "b_k_norm_g",
           "c_w_in", "c_q_norm_g", "c_k_norm_g", "c_lam", "c_subln_g")


def kernel(**inputs):
    x = np.asarray(inputs["x"], dtype=np.float32)
    mem = np.asarray(inputs["mem"], dtype=np.float32)
    B, S, _ = x.shape
    layers = [(i, i % 3, i // 3) for i in range(4)]
    nc = build(S, layers)
    cbc, cfc, selc, ohc = make_consts()
    params = {k: np.ascontiguousarray(np.asarray(inputs[k], dtype=np.float32)) for k in _PARAMS}
    in_maps = []
    for b in range(B):
        m = dict(params)
        m["x"] = np.ascontiguousarray(x[b])
        m["mem"] = np.ascontiguousarray(mem[b])
        m["cbf"] = cbc
        m["cf32"] = cfc
        m["selc"] = selc
        m["ohc"] = ohc
        in_maps.append(m)
    res = run_bass_kernel_spmd(nc, in_maps, core_ids=list(range(B)))
    return np.stack([np.asarray(r["y"], dtype=np.float32) for r in res.results], axis=0)
```

```python
import math
from contextlib import ExitStack

import ml_dtypes
import numpy as np

import concourse.bass as bass
import concourse.mybir as mybir
from concourse.bass_utils import run_bass_kernel_spmd

F32 = mybir.dt.float32
BF16 = mybir.dt.bfloat16
AF = mybir.ActivationFunctionType
ALU = mybir.AluOpType

D = 2048
NCH = 16
NMEM = 256
MIXW = 1536
MEMW = 512
EPS = 1e-6
ENGS = ("sp", "pe", "act", "dve", "pool")
SAME_RAW = True


class Buf:
    def __init__(self, multi=False, excl=False):
        self.w = {}
        self.r = {}
        self.multi = multi
        self.excl = excl


class Chan:
    def __init__(self, sem):
        self.sem = sem
        self.total = 0


class Prog:
    def __init__(self, nc, es):
        self.nc = nc
        self.es = es
        self.recs = {e: [] for e in ENGS}
        self.cnt = {e: 0 for e in ENGS}
        self.seen = {e: {} for e in ENGS}
        self.sem = {e: es.enter_context(nc.semaphore("s_" + e)) for e in ENGS}
        self.chans = []
        self.bufs = []
        self.nchan = 0
        self.pre_barrier = None

    def buf(self, multi=False, excl=False):
        b = Buf(multi, excl)
        self.bufs.append(b)
        return b

    def chan(self):
        self.nchan += 1
        c = Chan(self.es.enter_context(self.nc.semaphore("c%d" % self.nchan)))
        self.chans.append(c)
        return c

    def _semh(self, key):
        return key.sem if isinstance(key, Chan) else self.sem[key]

    def _need(self, eng, waits, ev, kind, is_dma=False):
        key, val, src = ev
        if src == eng and not is_dma:
            if eng == "pe" or not SAME_RAW:
                return
        if self.seen[eng].get(key, 0) >= val:
            return
        self.seen[eng][key] = val
        waits.append((key, val))

    def op(self, eng, fn, reads=(), writes=(), chan=None):
        waits = []
        isd = chan is not None
        for b in reads:
            for ev in b.w.values():
                self._need(eng, waits, ev, "raw", isd)
            if b.excl:
                for ev in b.r.values():
                    if ev[2] != eng:
                        self._need(eng, waits, ev, "rar", isd)
        for b in writes:
            if not b.multi:
                for ev in b.w.values():
                    self._need(eng, waits, ev, "waw", isd)
            for ev in b.r.values():
                self._need(eng, waits, ev, "war", isd)
        if chan is None:
            self.cnt[eng] += 1
            ev = (eng, self.cnt[eng], eng)
            inc = (eng, 1)
        else:
            chan.total += 16
            ev = (chan, chan.total, "dma")
            inc = (chan, 16)
        self.recs[eng].append((waits, fn, inc))
        for b in reads:
            b.r[ev[0]] = ev
        for b in writes:
            if b.multi:
                b.w[ev[0]] = ev
            else:
                b.w = {ev[0]: ev}
                b.r = {}

    def barrier(self):
        if self.pre_barrier is not None:
            self.pre_barrier()
        for e in ENGS:
            waits = []
            for e2 in ENGS:
                if e2 != e and self.cnt[e2] > self.seen[e].get(e2, 0):
                    self.seen[e][e2] = self.cnt[e2]
                    waits.append((e2, self.cnt[e2]))
            for c in self.chans:
                if c.total > self.seen[e].get(c, 0):
                    self.seen[e][c] = c.total
                    waits.append((c, c.total))
            if waits:
                self.recs[e].append((waits, None, None))
        for b in self.bufs:
            b.w = {}
            b.r = {}

    def flush(self):
        with self.nc.Block() as blk:
            for eng, deco in (("sp", blk.sync), ("pe", blk.tensor), ("act", blk.scalar),
                              ("dve", blk.vector), ("pool", blk.gpsimd)):
                recs = self.recs[eng]
                if not recs:
                    continue

                def body(e, recs=recs):
                    for waits, fn, inc in recs:
                        for key, val in waits:
                            e.wait_ge(self._semh(key), val)
                        if fn is None:
                            continue
                        ins = fn(e)
                        ins.then_inc(self._semh(inc[0]), inc[1])

                deco(body)
                self.recs[eng] = []


def _t5_bucket(rel):
    nb = 16
    max_exact = 8
    ret = np.where(rel > 0, nb, 0)
    n = np.abs(rel)
    nf = np.maximum(n, 1).astype(np.float32)
    large = max_exact + (np.log(nf / max_exact) / math.log(128 / max_exact) * (nb - max_exact)).astype(np.int32)
    large = np.minimum(large, nb - 1)
    return ret + np.where(n < max_exact, n, large)


CB_ID, CB_ONES, CB_BONES, CB_MF, CB_MD = 0, 128, 256, 384, 512
NCB = 640
CF_ID, CF_MW, CF_ONES = 0, 128, 256
NCF = 384


def make_consts():
    cb = np.zeros((128, NCB), np.float32)
    i = np.arange(128)
    cb[:, CB_ID:CB_ID + 128] = np.eye(128)
    cb[:, CB_ONES:CB_ONES + 128] = 1.0
    cb[:, CB_BONES:CB_BONES + 128] = (i[:, None] // 64 == i[None, :] // 64)
    cb[:, CB_MF:CB_MF + 128] = (i[:, None] <= i[None, :])
    cb[:, CB_MD:CB_MD + 128] = (i[:, None] // 64 <= i[None, :] // 64)
    sel = np.zeros((128, 12 * 128), np.float32)
    for h in range(12):
        for g in range(3):
            sel[32 * g + h, 128 * h:128 * (h + 1)] = 1.0
    cf = np.zeros((128, NCF), np.float32)
    cf[:, CF_ID:CF_ID + 128] = np.eye(128)
    cf[:, CF_MW:CF_MW + 128] = (i[:, None] // 64 >= i[None, :] // 64)
    cf[:, CF_ONES:CF_ONES + 128] = 1.0
    j = np.arange(383)
    bk = _t5_bucket(127 - j)
    oh = np.zeros((32, 384), np.float32)
    for b in range(32):
        oh[b, 0:383] = (bk == b)
    return cb.astype(ml_dtypes.bfloat16), cf, sel.astype(ml_dtypes.bfloat16), oh


class K:
    pass


def build(S, layers, debug_out=None):
    NT = S // 128
    NG = S // 512
    nc = bass.Bass("TRN2", target_bir_lowering=False)
    dt = nc.dram_tensor

    def din(name, shape, dtype=F32):
        return dt(name, list(shape), dtype, kind="ExternalInput").ap()

    x_in = din("x", [S, D])
    mem_in = din("mem", [NMEM, D])
    mem_norm_g = din("mem_norm_g", [D])
    rel_bias = din("rel_bias", [32, 12])
    norm_g = din("norm_g", [4, D])
    w_mem_kv = din("w_mem_kv", [4, D, 1024])
    mem_q_norm_g = din("mem_q_norm_g", [4, 128])
    mem_k_norm_g = din("mem_k_norm_g", [4, 128])
    w_out = din("w_out", [4, D, D])
    a_w_in = din("a_w_in", [2, D, 5632])
    a_ln_g = din("a_ln_g", [2, MIXW])
    a_ln_b = din("a_ln_b", [2, MIXW])
    a_w_s = din("a_w_s", [2, 12, 128, 128])
    a_b_s = din("a_b_s", [2, 12, 128])
    b_w_in = din("b_w_in", [1, D, 7180])
    b_b_f = din("b_b_f", [1, 12])
    b_q_norm_g = din("b_q_norm_g", [1, 128])
    b_k_norm_g = din("b_k_norm_g", [1, 128])
    c_w_in = din("c_w_in", [1, D, 7168])
    c_q_norm_g = din("c_q_norm_g", [1, 64])
    c_k_norm_g = din("c_k_norm_g", [1, 64])
    c_lam = din("c_lam", [1, 4, 64])
    c_subln_g = din("c_subln_g", [1, 128])
    cbf_in = din("cbf", [128, NCB], BF16)
    cf_in = din("cf32", [128, NCF])
    sel_in = din("selc", [128, 12 * 128], BF16)
    oh_in = din("ohc", [32, 384])
    y_out = dt("y", [S, D], F32, kind="ExternalOutput").ap()
    xs1 = dt("xscr1", [S, D], F32, kind="Internal").ap()
    xs2 = dt("xscr2", [S, D], F32, kind="Internal").ap()
    brT = dt("brT", [D, S], BF16, kind="Internal").ap()
    svT = dt("svT", [MIXW, S], BF16, kind="Internal").ap()
    trd = dt("trd", [12, 384], F32, kind="Internal").ap()

    es = ExitStack()
    with es:
        P = Prog(nc, es)

        uid = [0]

        def sb(stack, name, shape, dtype):
            uid[0] += 1
            return stack.enter_context(nc.sbuf_tensor("%s_%d" % (name, uid[0]), list(shape), dtype))

        cb = sb(es, "cb", [128, NCB], BF16)
        cf = sb(es, "cf", [128, NCF], F32)
        b_cb, b_cf, b_memT = P.buf(), P.buf(), P.buf()
        memTd = dt("memTd", [128, NCH * NMEM], BF16, kind="Internal").ap()
        b_memTd = P.buf(multi=True)
        psbig = [es.enter_context(nc.psum_tensor("ps%d" % i, [128, 1024], F32)) for i in range(4)]
        psum = [psbig[i // 2][:, (i % 2) * 512:(i % 2 + 1) * 512] for i in range(8)]
        b_ps = [P.buf(excl=True) for _ in range(8)]
        c_const = P.chan()
        ident = cb[:, CB_ID:CB_ID + 128]
        ones = cb[:, CB_ONES:CB_ONES + 128]
        bones = cb[:, CB_BONES:CB_BONES + 128]
        maskF = cb[:, CB_MF:CB_MF + 128]
        maskD = cb[:, CB_MD:CB_MD + 128]
        ident32 = cf[:, CF_ID:CF_ID + 128]
        maskW = cf[:, CF_MW:CF_MW + 128]
        ones32 = cf[:, CF_ONES:CF_ONES + 128]
        b_brT = P.buf(multi=True)
        b_svT = P.buf(multi=True)
        b_x = {id(t): P.buf(multi=True) for t in (x_in, xs1, xs2, y_out)}
        bx = lambda t: b_x[id(t)]

        P.op("sp", lambda e: e.dma_start(out=cb[:], in_=cbf_in[:, :]), writes=[b_cb], chan=c_const)
        P.op("sp", lambda e: e.dma_start(out=cf[:], in_=cf_in[:, :]), writes=[b_cf], chan=c_const)

        def mm(out, lhsT, rhs, start, stop, reads, writes):
            P.op("pe", lambda e: e.matmul(out, lhsT, rhs, start=start, stop=stop), reads=reads, writes=writes)

        def act(out, in_, func, reads, writes, bias=None, scale=None):
            kw = {}
            if bias is not None:
                kw["bias"] = bias
            if scale is not None:
                kw["scale"] = scale
            P.op("act", lambda e: e.activation(out=out, in_=in_, func=func, **kw), reads=reads, writes=writes)

        def tt(eng, out, in0, in1, op, reads, writes):
            P.op(eng, lambda e: e.tensor_tensor(out=out, in0=in0, in1=in1, op=op), reads=reads, writes=writes)

        def ts(eng, out, in0, s1, s2, op0, op1, reads, writes):
            if op1 is None:
                P.op(eng, lambda e: e.tensor_scalar(out=out, in0=in0, scalar1=s1, scalar2=None, op0=op0),
                     reads=reads, writes=writes)
            else:
                P.op(eng, lambda e: e.tensor_scalar(out=out, in0=in0, scalar1=s1, scalar2=s2, op0=op0, op1=op1),
                     reads=reads, writes=writes)

        def stt(out, in0, scalar, in1, op0, op1, reads, writes):
            P.op("dve", lambda e: e.scalar_tensor_tensor(out=out, in0=in0, scalar=scalar, in1=in1, op0=op0, op1=op1),
                 reads=reads, writes=writes)

        def recip(out, in_, reads, writes):
            P.op("dve", lambda e: e.reciprocal(out=out, in_=in_), reads=reads, writes=writes)

        deferred = []

        def run_deferred():
            while deferred:
                deferred.pop(0)()

        PROJ = [[5, 6, 7]]
        P.pre_barrier = run_deferred

        def copy(eng, out, in_, reads, writes):
            if eng == "act":
                P.op("act", lambda e: e.copy(out=out, in_=in_), reads=reads, writes=writes)
            else:
                P.op(eng, lambda e: e.tensor_copy(out=out, in_=in_), reads=reads, writes=writes)

        def dma(eng, out, in_, reads, writes, chan, **kw):
            P.op(eng, lambda e: e.dma_start(out=out, in_=in_, **kw), reads=reads, writes=writes, chan=chan)

        def memset(eng, ap, val, writes):
            P.op(eng, lambda e: e.memset(ap, val), writes=writes)

        class Rot:
            def __init__(self, stack, name, n, shape, dtype, chans=False):
                self.t = [sb(stack, "%s%d" % (name, i), shape, dtype) for i in range(n)]
                self.b = [P.buf() for _ in range(n)]
                self.c = [P.chan() for _ in range(n)] if chans else [None] * n
                self.i = 0
                self.n = n

            def next(self):
                i = self.i
                self.i = (i + 1) % self.n
                return self.t[i], self.b[i], self.c[i]

        ps_rr = [0]

        def ps_next(cands):
            i = cands[ps_rr[0] % len(cands)]
            ps_rr[0] += 1
            return psum[i], b_ps[i]

        chan_pool = {}

        def get_chans(name, n):
            if name not in chan_pool:
                chan_pool[name] = [P.chan() for _ in range(n)]
            return chan_pool[name]

        class RotC(Rot):
            def __init__(self, stack, name, n, shape, dtype):
                self.t = [sb(stack, "%s%d" % (name, i), shape, dtype) for i in range(n)]
                self.b = [P.buf() for _ in range(n)]
                self.c = get_chans(name, n)
                self.i = 0
                self.n = n

        def p1_norm_transpose(stack_tensors, src, ntok, gsrc_ap, dstT, dst_bufs_for_tile, src_buf):
            xs, gfm, stat = stack_tensors
            gt, gb, gc = gfm
            dma("sp", gt[:], gsrc_ap.rearrange("(c p) -> p c", p=128), [], [gb], gc, allow_slow_non_contiguous=True)
            for t in range(ntok // 128):
                xt, xb, xc = xs.next()
                st, stb, _ = stat.next()
                dma("sp", xt[:], src[t * 128:(t + 1) * 128, :], [src_buf], [xb], xc)
                jt, jb, _ = xs.junk
                P.op("act", lambda e, jt=jt, xt=xt, st=st: e.activation(out=jt[:], in_=xt[:], func=AF.Square,
                                                                         accum_out=st[:, 0:1]),
                     reads=[xb], writes=[jb, stb])
                act(st[:, 1:2], st[:, 0:1], AF.Ln, [stb, b_eps], [stb], bias=EPS_AP[0], scale=1.0 / D)
                act(st[:, 2:3], st[:, 1:2], AF.Exp, [stb], [stb], scale=-0.5)
                ts("dve", xt[:], xt[:], st[:, 2:3], None, ALU.mult, None, [xb, stb], [xb])
                for jb4 in range(NCH // 4):
                    pt, pb = ps_next([4, 5, 6, 7])
                    for c4 in range(4):
                        c = jb4 * 4 + c4
                        P.op("pe", lambda e, pt=pt, c4=c4, c=c, xt=xt: e.transpose(
                            pt[:, c4 * 128:(c4 + 1) * 128], xt[:, c * 128:(c + 1) * 128], ident32),
                            reads=[xb, b_cf], writes=[pb])
                    tt("dve", dstT[:, jb4 * 4:(jb4 + 1) * 4, t * 128:(t + 1) * 128],
                       pt[:, :].rearrange("p (c n) -> p c n", c=4),
                       gt[:, jb4 * 4:(jb4 + 1) * 4].unsqueeze(2).broadcast_to([128, 4, 128]),
                       ALU.mult, [pb, gb], [dst_bufs_for_tile(t)])

        epst = sb(es, "epst", [128, 4], F32)
        b_eps = P.buf()
        memset("dve", epst[:, 0:1], EPS, [b_eps])
        memset("dve", epst[:, 1:2], 0.0, [b_eps])
        memset("dve", epst[:, 2:3], 1.0, [b_eps])
        EPS_AP = [epst[:, 0:1], epst[:, 1:2], epst[:, 2:3]]

        def load_w(rot, wsrc, col0, ncols=128, cols_out=None):
            wt, wb, wc = rot.next()
            o = wt[:, :, 0:ncols] if cols_out is None else cols_out(wt)
            dma("pool", o, wsrc[:, col0:col0 + ncols].rearrange("(c p) n -> p c n", p=128), [], [wb], wc)
            return wt, wb

        def proj_fm(wt, wb, hT, hbufs, tg, ntok=512):
            pt, pb = ps_next(PROJ[0])
            for c in range(NCH):
                mm(pt[:, 0:ntok], wt[:, c, :], hT[:, c, tg * 512:tg * 512 + ntok], c == 0, c == NCH - 1,
                   [wb, hbufs[tg]], [pb])
            return pt, pb

        def rms_fm_epilogue(tmp, pt, pb, gcol, gbuf, out_ap, out_buf, ntok=512, half=False):
            zq, zb, _ = tmp["zq"].next()
            copy("dve", zq[:, 0:ntok], pt[:, 0:ntok], [pb], [zb])
            rms_from_sbuf(tmp, zq, zb, gcol, gbuf, out_ap, out_buf, ntok, half, defer=True)

        def rms_from_sbuf(tmp, zq, zb, gcol, gbuf, out_ap, out_buf, ntok=512, half=False, defer=False):
            sq, sqb, _ = tmp["sq"].next()
            rs, rsb, _ = tmp["rs"].next()
            tt("pool", sq[:, 0:ntok], zq[:, 0:ntok], zq[:, 0:ntok], ALU.mult, [zb], [sqb])

            def part2():
                p2, p2b = ps_next(PROJ[0])
                mm(p2[:, 0:ntok], bones if half else ones, sq[:, 0:ntok], True, True, [sqb, b_cb], [p2b])
                act(rs[:, 0:ntok], p2[:, 0:ntok], AF.Ln, [p2b, b_eps], [rsb], bias=EPS_AP[0],
                    scale=1.0 / (64 if half else 128))
                act(rs[:, 0:ntok], rs[:, 0:ntok], AF.Exp, [rsb], [rsb], scale=-0.5)
                stt(out_ap, zq[:, 0:ntok], gcol, rs[:, 0:ntok], ALU.mult, ALU.mult, [zb, rsb, gbuf], [out_buf])

            if defer:
                run_deferred()
                deferred.append(part2)
            else:
                part2()

        GATE_PRE = [None]

        def attention(T, nqb, qT, qbufs, kfn, vfn, nkt_fn, collo_fn, bias_fn, mask_fn, aux, nm,
                      finish, gate_w=None, mid_cb=None):
            run_deferred()
            if nm == 1:
                accs = [[(psum[0], b_ps[0]), (psum[1], b_ps[1])], [(psum[5], b_ps[5]), (psum[6], b_ps[6])]]
                sbanks = [[(psum[2], b_ps[2])], [(psum[3], b_ps[3])], [(psum[4], b_ps[4])]]
            else:
                accs = [[(psum[0], b_ps[0]), (psum[1], b_ps[1]), (psum[2], b_ps[2]), (psum[3], b_ps[3])]]
                sbanks = [[(psum[4], b_ps[4]), (psum[5], b_ps[5])], [(psum[6], b_ps[6]), (psum[7], b_ps[7])]]
            LOOK = len(sbanks) - 1
            for qb in range(nqb):
                acc = accs[qb % len(accs)]
                q0 = qb * 512
                pairs = list(range(nkt_fn(qb)))
                npairs = len(pairs)
                sstate = {}

                def emit_s(i):
                    kt = pairs[i]
                    lo = collo_fn(qb, kt)
                    sb_ = sbanks[i % len(sbanks)]
                    for m in range(nm):
                        st_, stb_ = sb_[m]
                        lhsT, psl, kb = kfn(kt, m)
                        mm(st_[:, lo:512], lhsT, qT[psl, q0 + lo:q0 + 512], True, aux is None,
                           kb + [qbufs[qb]], [stb_])
                        if aux is not None:
                            sel_ap, rows, rbuf = aux
                            mm(st_[:, lo:512], sel_ap, rows[:, q0 + lo:q0 + 512], False, True,
                               [b_cb, rbuf], [stb_])
                    sstate[i] = (kt, lo, sb_)

                def emit_pv(i):
                    kt, lo, sb_ = sstate.pop(i)
                    if nm == 2:
                        pT2, pT2b, _ = T["pT2"].next()
                        big = psbig[2 + (i % 2)]
                        act(pT2[:, :, lo:512], big[:, :].rearrange("p (m n) -> p m n", m=2)[:, :, lo:512], AF.Exp,
                            [sb_[0][1], sb_[1][1]], [pT2b])
                    for m in range(nm):
                        st_, stb_ = sb_[m]
                        if nm == 2:
                            pT, pTb = pT2[:, m, :], pT2b
                        else:
                            pT, pTb, _ = T["pT"].next()
                        bias_ap, bias_bufs = bias_fn(kt)
                        if nm == 2:
                            pass
                        elif bias_ap is None:
                            act(pT[:, lo:512], st_[:, lo:512], AF.Exp, [stb_], [pTb])
                        else:
                            act(pT[:, lo:512], st_[:, lo:512], AF.Exp, [stb_] + bias_bufs, [pTb], bias=bias_ap)
                        if nm == 1:
                            for (c0, mk_ap, mk_bufs) in mask_fn(qb, kt, m):
                                tt("pool", pT[:, c0:c0 + 128], pT[:, c0:c0 + 128], mk_ap, ALU.mult,
                                   [pTb] + mk_bufs, [pTb])
                        elif m == 0:
                            for (c0, w_, mk_ap, mk_bufs) in mask_fn(qb, kt, m):
                                tt("dve", pT2[:, :, c0:c0 + w_], pT2[:, :, c0:c0 + w_],
                                   mk_ap.unsqueeze(1).broadcast_to([128, 2, w_]), ALU.mult,
                                   [pT2b] + mk_bufs, [pT2b])
                        vl, vb = vfn(kt)
                        num, numb = acc[m]
                        den, denb = acc[nm + m]
                        mm(num[:, lo:512], vl, pT[:, lo:512], i == 0, i == npairs - 1, vb + [pTb], [numb])
                        mm(den[:, lo:512], ones, pT[:, lo:512], i == 0, i == npairs - 1, [b_cb, pTb], [denb])

                for i in range(min(LOOK, npairs)):
                    emit_s(i)
                for i in range(npairs):
                    if i + LOOK < npairs:
                        emit_s(i + LOOK)
                    emit_pv(i)
                    if i == min(8, npairs - 1):
                        save = PROJ[0]
                        PROJ[0] = [[2, 3, 4][i % 3]] if nm == 1 else [4 + 2 * (i % 2)]
                        run_deferred()
                        if gate_w is not None:
                            PROJ[0] = [7] if nm == 1 else [5 + 2 * (i % 2)]
                            GATE_PRE[0] = gate_proj(T, gate_w, qb)
                        PROJ[0] = save
                        if qb == 0 and mid_cb is not None:
                            mid_cb()
                finish(qb, acc)

        def gate_proj(T, gw, qb):
            wt, wb = gw
            pt, pb = proj_fm(wt, wb, T["hT"], T["h_b"], qb)
            gt, gtb, _ = T["g"].next()
            e_, eb, _ = T["rs"].next()
            xs_, xsb, _ = T["gx"].next()
            copy("act", xs_[:], pt[:], [pb], [xsb])
            act(e_[:], xs_[:], AF.Exp, [xsb], [eb], scale=-1.0)
            act(e_[:], e_[:], AF.Ln, [eb, b_eps], [eb], bias=EPS_AP[2])
            act(e_[:], e_[:], AF.Exp, [eb], [eb], scale=-1.0)
            tt("pool", gt[:], xs_[:], e_[:], ALU.mult, [xsb, eb], [gtb])
            return gt, gtb

        def store_branch(T, o_ap, o_bufs, gw, cb_idx, qb, pre=None):
            gt, gtb = gate_proj(T, gw, qb) if pre is None else pre
            stg, stgb, stgc = T["stg"].next()
            tt("pool", stg[:], o_ap, gt[:], ALU.mult, o_bufs + [gtb], [stgb])
            dma("sp", brT[cb_idx * 128:(cb_idx + 1) * 128, qb * 512:(qb + 1) * 512], stg[:], [stgb], [b_brT], stgc)

        def finish_simple(T, gw, cb_idx):
            def fin(qb, acc):
                (num, numb), (den, denb) = acc
                pre = GATE_PRE[0]
                rd, rdb, _ = T["rs"].next()
                o, ob, _ = T["zq"].next()
                recip(rd[:], den[:], [denb], [rdb])
                tt("dve", o[:], num[:], rd[:], ALU.mult, [numb, rdb], [ob])
                store_branch(T, o[:], [ob], gw, cb_idx, qb, pre=pre)
            return fin

        def mem_kv(T, L):
            mkT, mv, memT = T["mkT"], T["mv"], T["memT"]
            gk = T["gsm"]
            for hm in range(4):
                wt, wb = load_w(T["w"], w_mem_kv[L], hm * 128)
                pt, pb = ps_next(PROJ[0])
                for c in range(NCH):
                    mm(pt[:, 0:NMEM], wt[:, c, :], memT[:, c, :], c == 0, c == NCH - 1, [wb, b_memT], [pb])
                rms_fm_epilogue(T, pt, pb, gk[:, 0:1], T["gsm_b"], mkT[:, hm, :], T["mkT_b"], ntok=NMEM)
            for hm in range(4):
                wt, wb = load_w(T["w"], w_mem_kv[L], MEMW + hm * 128)
                pt, pb = ps_next(PROJ[0])
                for mt in range(2):
                    for c in range(NCH):
                        mm(pt[:, mt * 128:(mt + 1) * 128], memT[:, c, mt * 128:(mt + 1) * 128], wt[:, c, :],
                           c == 0, c == NCH - 1, [wb, b_memT], [pb])
                copy("act", mv[:, :, hm * 128:(hm + 1) * 128], pt[:, 0:256].rearrange("p (t n) -> p t n", t=2),
                     [pb], [T["mv_b"]])

        def mem_heads(st, T, L, wsrc, memq_off, gate_off, hT, hbufs):
            alloc_attn(st, T)
            T["memT"] = sb(st, "memT", [128, NCH, NMEM], BF16)
            T["mkT"] = sb(st, "mkT", [128, 4, NMEM], BF16)
            T["mkT_b"] = P.buf()
            T["mv"] = sb(st, "mv", [128, 2, 512], BF16)
            T["mv_b"] = P.buf()
            dma("sp", T["memT"][:].rearrange("p c n -> p (c n)"), memTd[:, :], [b_memTd], [b_memT], get_chans("memT", 1)[0])
            mem_kv(T, L)
            nxtm = {0: load_w(T["w"], wsrc, memq_off)}

            def issue_m(hh):
                nxtm[hh] = load_w(T["w"], wsrc, memq_off + hh * 128)

            for hm in range(4):
                wt, wb = nxtm.pop(hm)
                for tg in range(NG):
                    pt, pb = proj_fm(wt, wb, hT, hbufs, tg)
                    rms_fm_epilogue(T, pt, pb, T["gsm"][:, 1:2], T["gsm_b"],
                                    T["qT"][:, tg * 512:(tg + 1) * 512], T["q_b"][tg])
                gw = load_w(T["w"], wsrc, gate_off + MIXW + hm * 128)
                attention(T, NG, T["qT"], T["q_b"],
                          kfn=lambda kt, m, hm=hm: (T["mkT"][:, hm, kt * 128:(kt + 1) * 128], slice(0, 128), [T["mkT_b"]]),
                          vfn=lambda kt, hm=hm: (T["mv"][:, kt, hm * 128:(hm + 1) * 128], [T["mv_b"]]),
                          nkt_fn=lambda qb: 2, collo_fn=lambda qb, kt: 0,
                          bias_fn=lambda kt: (None, []), mask_fn=lambda qb, kt, m: [],
                          aux=None, nm=1, finish=finish_simple(T, gw, 12 + hm), gate_w=gw,
                          mid_cb=(lambda hm=hm: issue_m(hm + 1)) if hm < 3 else None)

        def load_small(T, L, kind, j):
            g = T["gsm"]
            gb = T["gsm_b"]
            gc = T["gsm_c"]
            tmp = T["gsm_raw"]
            dma("sp", tmp[:, 0:1], mem_k_norm_g[L].unsqueeze(1), [], [gb], gc)
            dma("sp", tmp[:, 1:2], mem_q_norm_g[L].unsqueeze(1), [], [gb], gc)
            if kind == 1:
                dma("sp", tmp[:, 2:3], b_q_norm_g[j].unsqueeze(1), [], [gb], gc)
                dma("sp", tmp[:, 3:4], b_k_norm_g[j].unsqueeze(1), [], [gb], gc)
            if kind == 2:
                for hh in range(2):
                    dma("sp", tmp[hh * 64:(hh + 1) * 64, 2:3], c_q_norm_g[j].unsqueeze(1), [], [gb], gc)
                    dma("sp", tmp[hh * 64:(hh + 1) * 64, 3:4], c_k_norm_g[j].unsqueeze(1), [], [gb], gc)
                dma("sp", tmp[:, 4:5], c_subln_g[j].unsqueeze(1), [], [gb], gc)
            copy("dve", g[:, 0:1], tmp[:, 0:1], [gb], [gb])
            ts("dve", g[:, 1:2], tmp[:, 1:2], 128 ** -0.5, None, ALU.mult, None, [gb], [gb])
            if kind == 1:
                ts("dve", g[:, 2:3], tmp[:, 2:3], 128 ** -0.5, None, ALU.mult, None, [gb], [gb])
                copy("dve", g[:, 3:4], tmp[:, 3:4], [gb], [gb])
            if kind == 2:
                ts("dve", g[:, 2:3], tmp[:, 2:3], 64 ** -0.5, None, ALU.mult, None, [gb], [gb])
                copy("dve", g[:, 3:4], tmp[:, 3:4], [gb], [gb])
                lam_init = 0.8 - 0.6 * math.exp(-0.3 * L)
                ts("dve", g[:, 4:5], tmp[:, 4:5], 1.0 - lam_init, None, ALU.mult, None, [gb], [gb])

        def alloc_attn(stk, T, nm=1):
            T["qT"] = sb(stk, "qT", [128, S], BF16)
            T["q_b"] = [P.buf() for _ in range(NG)]
            if nm == 1:
                T["pT"] = Rot(stk, "pT", 4, [128, 512], BF16)
            else:
                T["pT2"] = Rot(stk, "pT2", 3, [128, 2, 512], BF16)
                T["on"] = Rot(stk, "on", 2, [128, 512], F32)

        def common_tensors(st, kind):
            T = {}
            T["hT"] = sb(st, "hT", [128, NCH, S], BF16)
            T["h_b"] = [P.buf() for _ in range(NG)]
            T["w"] = RotC(st, "w", 3, [128, NCH, 128], BF16)
            T["g"] = Rot(st, "g", 3 if kind == 2 else 2, [128, 512], BF16)
            T["gx"] = Rot(st, "gx", 2, [128, 512], BF16)
            T["zq"] = Rot(st, "zq", 3 if kind == 0 else 4, [128, 512], F32)
            T["sq"] = Rot(st, "sq", 2 if kind == 0 else 3, [128, 512], BF16)
            T["rs"] = Rot(st, "rs", 2 if kind == 0 else 3, [128, 512], F32)
            T["stg"] = RotC(st, "stg", 2, [128, 512], BF16)
            T["gsm"] = sb(st, "gsm", [128, 8], F32)
            T["gsm_raw"] = sb(st, "gsmr", [128, 8], F32)
            T["gsm_b"] = P.buf()
            T["gsm_c"] = get_chans("gsm", 1)[0]
            T["gfm"] = (sb(st, "gfm", [128, NCH], F32), P.buf(), get_chans("gfm", 1)[0])
            return T

        def p1_tensors(st):
            xs = RotC(st, "xs", 2, [128, D], F32)
            xs.junk = (sb(st, "xjunk", [128, D], BF16), P.buf(), None)
            stat = Rot(st, "stat", 2, [128, 4], F32)
            return xs, stat

        with ExitStack() as st:
            xs, stat = p1_tensors(st)
            gfm = (sb(st, "gfm0", [128, NCH], F32), P.buf(), get_chans("gfm", 1)[0])
            b_memsrc = P.buf()
            memT0 = sb(st, "memT0", [128, NCH, NMEM], BF16)
            p1_norm_transpose((xs, gfm, stat), mem_in, NMEM, mem_norm_g, memT0, lambda t: b_memT, b_memsrc)
            dma("sp", memTd[:, :], memT0[:].rearrange("p c n -> p (c n)"), [b_memT], [b_memTd], get_chans("memT", 1)[0])
            P.barrier()
            P.flush()

        def proj_v(T, wt, wb, vt, vbufs):
            for tb in range(NG):
                pt, pb = ps_next(PROJ[0])
                for t4 in range(4):
                    t = tb * 4 + t4
                    for c in range(NCH):
                        mm(pt[:, t4 * 128:(t4 + 1) * 128], T["hT"][:, c, t * 128:(t + 1) * 128], wt[:, c, :],
                           c == 0, c == NCH - 1, [wb, T["h_b"][tb]], [pb])
                copy("act", vt[:, tb * 4:(tb + 1) * 4, :], pt[:].rearrange("p (t n) -> p t n", t=4), [pb], [vbufs[tb]])

        def layer_b(st, T, L, j, wsrc, gate_off):
            alloc_attn(st, T)
            kT = sb(st, "kT", [128, S], BF16)
            k_b = [P.buf() for _ in range(NG)]
            vt = sb(st, "vt", [128, NT, 128], BF16)
            v_b = [P.buf() for _ in range(NG)]
            rows = sb(st, "rows", [128, S], BF16)
            rows_b = P.buf()
            selt = sb(st, "selt", [128, 12 * 128], BF16)
            dma("sp", selt[:], sel_in[:, :], [], [b_cb], get_chans("selt", 1)[0])
            negc = sb(st, "negc", [128, NT, 12], F32)
            negc_b = P.buf()
            sm = sb(st, "fsm", [128, 4], F32)
            sm_b = P.buf()
            smc = get_chans("fsm", 1)[0]
            memset("dve", sm[:], 0.0, [sm_b])
            memset("dve", sm[:, 2:3], 1.0, [sm_b])
            for g in range(3):
                dma("sp", sm[32 * g:32 * g + 12, 0:1], b_b_f[j].unsqueeze(1), [], [sm_b], smc)
            wt, wb, wc = T["w"].next()
            P.op("pool", lambda e: e.memset(wt[:], 0.0), writes=[wb])
            for g in range(3):
                dma("pool", wt[:, :, 32 * g:32 * g + 12],
                    wsrc[:, 3 * MIXW:3 * MIXW + 12].rearrange("(c p) n -> p c n", p=128), [], [wb], wc)
            for tg in range(NG):
                pt, pb = proj_fm(wt, wb, T["hT"], T["h_b"], tg)
                ls, lsb, _ = T["zq"].next()
                cc, ccb, _ = T["rs"].next()
                act(ls[:], pt[:], AF.Sigmoid, [pb, sm_b], [lsb], bias=sm[:, 0:1])
                act(ls[:], ls[:], AF.Ln, [lsb], [lsb])
                init = 0.0 if tg == 0 else sm[:, 1:2]
                P.op("dve", lambda e, cc=cc, ls=ls, init=init: e.tensor_tensor_scan(
                    out=cc[:], data0=sm[:, 2:3].broadcast_to([128, 512]), data1=ls[:], initial=init,
                    op0=ALU.mult, op1=ALU.add), reads=[lsb, sm_b], writes=[ccb])
                copy("dve", sm[:, 1:2], cc[:, 511:512], [ccb], [sm_b])
                sl = slice(tg * 512, (tg + 1) * 512)
                copy("act", rows[:, sl], cc[:], [ccb], [rows_b])
                r1, r1b, _ = T["zq"].next()
                tt("dve", r1[:], cc[:], rows[:, sl], ALU.subtract, [ccb, rows_b], [r1b])
                md, mdb, _ = T["sq"].next()
                copy("act", md[:], r1[:], [r1b], [mdb])
                copy("pool", rows[32:64, sl], md[32:64, :], [mdb], [rows_b])
                tt("dve", r1[:], r1[:], md[:], ALU.subtract, [r1b, mdb], [r1b])
                copy("act", rows[64:96, sl], r1[64:96, :], [r1b], [rows_b])
                p2, p2b = ps_next(PROJ[0])
                for i4 in range(4):
                    P.op("pe", lambda e, p2=p2, cc=cc, i4=i4: e.transpose(
                        p2[:, i4 * 32:(i4 + 1) * 32], cc[0:32, i4 * 128:(i4 + 1) * 128], ident32[0:32, 0:32]),
                        reads=[ccb, b_cf], writes=[p2b])
                ts("dve", negc[:, tg * 4:(tg + 1) * 4, :],
                   p2[:, 0:128].rearrange("p (t n) -> p t n", t=4)[:, :, 0:12], -1.0, None, ALU.mult, None,
                   [p2b], [negc_b])
            nxt = {}

            def issue(hh):
                nxt[hh] = (load_w(T["w"], wsrc, hh * 128), load_w(T["w"], wsrc, MIXW + hh * 128))

            issue(0)
            for h in range(12):
                wq, wk = nxt.pop(h)
                for tg in range(NG):
                    pt, pb = proj_fm(wq[0], wq[1], T["hT"], T["h_b"], tg)
                    rms_fm_epilogue(T, pt, pb, T["gsm"][:, 2:3], T["gsm_b"],
                                    T["qT"][:, tg * 512:(tg + 1) * 512], T["q_b"][tg])
                for tg in range(NG):
                    pt, pb = proj_fm(wk[0], wk[1], T["hT"], T["h_b"], tg)
                    rms_fm_epilogue(T, pt, pb, T["gsm"][:, 3:4], T["gsm_b"],
                                    kT[:, tg * 512:(tg + 1) * 512], k_b[tg])
                wv = load_w(T["w"], wsrc, 2 * MIXW + h * 128)
                proj_v(T, wv[0], wv[1], vt, v_b)
                gw = load_w(T["w"], wsrc, gate_off + h * 128)
                sel_h = selt[:, 128 * h:128 * (h + 1)]
                attention(T, NG, T["qT"], T["q_b"],
                          kfn=lambda kt, m: (kT[:, kt * 128:(kt + 1) * 128], slice(0, 128), [k_b[kt // 4]]),
                          vfn=lambda kt: (vt[:, kt, :], [v_b[kt // 4]]),
                          nkt_fn=lambda qb: 4 * qb + 4,
                          collo_fn=lambda qb, kt: max(0, kt - 4 * qb) * 128,
                          bias_fn=lambda kt, h=h: (negc[:, kt, h:h + 1], [negc_b]),
                          mask_fn=lambda qb, kt, m: ([((kt - 4 * qb) * 128, maskF, [b_cb])] if kt >= 4 * qb else []),
                          aux=(sel_h, rows, rows_b), nm=1, finish=finish_simple(T, gw, h), gate_w=gw,
                          mid_cb=(lambda h=h: issue(h + 1)) if h < 11 else None)

        def layer_a(st, T, L, j, wsrc, gate_off):
            hT, hbufs = T["hT"], T["h_b"]
            addT = sb(st, "addT", [128, 12, 128], F32)
            wmT = sb(st, "wmT", [128, 12, 128], BF16)
            gb12 = sb(st, "gb12", [128, 24], F32)
            bc_b, wm_b = P.buf(), P.buf()
            bcc = get_chans("abc", 1)[0]
            dma("sp", gb12[:, 0:12], a_ln_g[j].rearrange("(g c) -> c g", c=128), [], [bc_b], bcc,
                allow_slow_non_contiguous=True)
            dma("sp", gb12[:, 12:24], a_ln_b[j].rearrange("(g c) -> c g", c=128), [], [bc_b], bcc,
                allow_slow_non_contiguous=True)
            with ExitStack() as st_t:
                bsbc = sb(st_t, "bsbc", [128, 12, 128], F32)
                wmf = Rot(st_t, "wmf", 2, [128, 128], F32)
                wl = RotC(st_t, "wsl", 2, [128, 128], F32)
                bs_b = P.buf()
                dma("sp", bsbc[:].rearrange("p g t -> p (g t)"),
                    a_b_s[j].rearrange("g t -> (g t)").partition_broadcast(128), [], [bs_b], get_chans("abs", 1)[0])
                for g in range(12):
                    wt_, wtb, wtc = wl.next()
                    dma("sp", wt_[:], a_w_s[j, g], [], [wtb], wtc)
                    tt("dve", wt_[:], wt_[:], maskW, ALU.mult, [wtb, b_cf], [wtb])
                    pt, pb = ps_next(PROJ[0])
                    P.op("pe", lambda e, pt=pt, wt_=wt_: e.transpose(pt[:, 0:128], wt_[:], ident32),
                         reads=[wtb, b_cf], writes=[pb])
                    copy("act", wmT[:, g, :], pt[:, 0:128], [pb], [wm_b])
                    p2, p2b = ps_next(PROJ[0])
                    mm(p2[:, 0:128], ones, wmT[:, g, :], True, True, [wm_b, b_cb], [p2b])
                    stt(addT[:, g, :], p2[:, 0:128], gb12[:, 12 + g:13 + g], bsbc[:, g, :], ALU.mult, ALU.add,
                        [p2b, bc_b, bs_b], [wm_b])
                P.barrier()
                P.flush()
            gv = [sb(st, "gv", [128, 4, MIXW], BF16) for _ in range(2)]
            gv_b = [[P.buf() for _ in range(4)] for _ in range(2)]
            vh = Rot(st, "vh", 3, [128, MIXW], BF16)
            lnst = Rot(st, "lnst", 4, [128, 24], F32)
            svs = RotC(st, "svs", 2, [128, 4, 128], BF16)
            svl = RotC(st, "svl", 2, [128, 512], BF16)

            def v_group(gi):
                gvt, gvb = gv[gi % 2], gv_b[gi % 2]
                for vb in range(12):
                    wt, wb = load_w(T["w"], wsrc, MIXW + vb * 128)
                    pt, pb = ps_next(PROJ[0])
                    for t4 in range(4):
                        t = gi * 4 + t4
                        for c in range(NCH):
                            mm(pt[:, t4 * 128:(t4 + 1) * 128], hT[:, c, t * 128:(t + 1) * 128], wt[:, c, :],
                               c == 0, c == NCH - 1, [wb, hbufs[gi]], [pb])
                    act(gvt[:, :, vb * 128:(vb + 1) * 128], pt[:].rearrange("p (t n) -> p t n", t=4),
                        AF.Gelu_apprx_tanh, [pb], gvb)

            def ln_group(gi):
                gvt, gvb = gv[gi % 2], gv_b[gi % 2]
                vts = []

                def ln_tile(t4):
                        ls_, lsb, _ = lnst.next()
                        for k3 in range(3):
                            P.op("dve", lambda e, ls_=ls_, k3=k3, t4=t4, gvt=gvt: e.bn_stats(
                                out=ls_[:, k3 * 6:(k3 + 1) * 6], in_=gvt[:, t4, k3 * 512:(k3 + 1) * 512]),
                                reads=[gvb[t4]], writes=[lsb])
                        P.op("dve", lambda e, ls_=ls_: e.bn_aggr(out=ls_[:, 18:20], in_=ls_[:, 0:18]),
                             reads=[lsb], writes=[lsb])
                        act(ls_[:, 20:21], ls_[:, 19:20], AF.Ln, [lsb, b_eps], [lsb], bias=EPS_AP[0])
                        act(ls_[:, 21:22], ls_[:, 20:21], AF.Exp, [lsb], [lsb], scale=-0.5)
                        vt_, vtb, _ = vh.next()
                        ts("dve", vt_[:], gvt[:, t4, :], ls_[:, 18:19], ls_[:, 21:22], ALU.subtract, ALU.mult,
                           [gvb[t4], lsb], [vtb])
                        vts.append((vt_, vtb))

                for t4 in range(3):
                    ln_tile(t4)
                for t4 in range(4):
                    if t4 == 1:
                        ln_tile(3)
                    t = gi * 4 + t4
                    vt_, vtb = vts[t4]
                    for g4 in range(3):
                        pt, pb = ps_next(PROJ[0])
                        for gi4 in range(4):
                            g = g4 * 4 + gi4
                            mm(pt[:, gi4 * 128:(gi4 + 1) * 128], vt_[:, g * 128:(g + 1) * 128], wmT[:, g, :],
                               True, True, [vtb, wm_b], [pb])
                        tm_, tmb, _ = T["zq"].next()
                        tt("dve", tm_[:].rearrange("p (g t) -> p g t", g=4), pt[:].rearrange("p (g t) -> p g t", g=4),
                           gb12[:, g4 * 4:(g4 + 1) * 4].unsqueeze(2).broadcast_to([128, 4, 128]), ALU.mult,
                           [pb, bc_b], [tmb])
                        sv_, svb, svc = svs.next()
                        tt("pool", sv_[:], tm_[:].rearrange("p (g t) -> p g t", g=4), addT[:, g4 * 4:(g4 + 1) * 4, :],
                           ALU.add, [tmb, wm_b], [svb])
                        dma("sp", svT[g4 * 512:(g4 + 1) * 512, t * 128:(t + 1) * 128].rearrange("(g c) n -> c g n", c=128),
                            sv_[:], [svb], [b_svT], svc)

            v_group(0)
            for gi in range(1, NG):
                v_group(gi)
                ln_group(gi - 1)
            ln_group(NG - 1)
            wu_next = load_w(T["w"], wsrc, 0)
            for g in range(12):
                wu = wu_next
                gw = load_w(T["w"], wsrc, gate_off + g * 128)
                for tg in range(NG):
                    if tg == NG // 2 and g < 11:
                        wu_next = load_w(T["w"], wsrc, (g + 1) * 128)
                    sl_, slb, slc = svl.next()
                    dma("sp", sl_[:], svT[g * 128:(g + 1) * 128, tg * 512:(tg + 1) * 512], [b_svT], [slb], slc)
                    pt, pb = proj_fm(wu[0], wu[1], hT, hbufs, tg)
                    u_, ub, _ = T["zq"].next()
                    act(u_[:], pt[:], AF.Gelu_apprx_tanh, [pb], [ub])
                    tt("pool", u_[:], u_[:], sl_[:], ALU.mult, [ub, slb], [ub])
                    store_branch(T, u_[:], [ub], gw, g, tg)

        def layer_c(st, T, L, j, wsrc, gate_off):
            Mall = sb(st, "Mall", [128, 12, 256], BF16)
            M_b = P.buf()
            csm = sb(st, "csm", [128, 40], F32)
            csm_b = P.buf()
            cc_ = get_chans("csm", 1)[0]
            lam_init = 0.8 - 0.6 * math.exp(-0.3 * L)
            dma("sp", csm[:, 0:12], rel_bias[15].partition_broadcast(128), [], [csm_b], cc_)
            ts("dve", csm[:, 12:24], csm[:, 0:12], -1.0, None, ALU.mult, None, [csm_b], [csm_b])
            with ExitStack() as st_t:
                tb_ = P.buf()
                lamt = sb(st_t, "lamt", [128, 256], F32)
                Ft = sb(st_t, "Ft", [128, 12, 256], F32)
                rb32 = sb(st_t, "rb32", [32, 12], F32)
                oht = sb(st_t, "oht", [32, 384], F32)
                dma("sp", oht[:], oh_in[:, :], [], [tb_], get_chans("oht", 1)[0])
                trs = sb(st_t, "trs", [12, 384], F32)
                tcn = get_chans("ctmp", 1)[0]
                dma("sp", lamt[:], c_lam[j].rearrange("a d -> (a d)").partition_broadcast(128), [], [tb_], tcn)
                tt("dve", lamt[:, 0:64], lamt[:, 0:64], lamt[:, 64:128], ALU.mult, [tb_], [tb_])
                tt("dve", lamt[:, 128:192], lamt[:, 128:192], lamt[:, 192:256], ALU.mult, [tb_], [tb_])
                P.op("dve", lambda e: e.reduce_sum(out=csm[:, 24:25], in_=lamt[:, 0:64], axis=mybir.AxisListType.X),
                     reads=[tb_], writes=[csm_b])
                P.op("dve", lambda e: e.reduce_sum(out=csm[:, 25:26], in_=lamt[:, 128:192], axis=mybir.AxisListType.X),
                     reads=[tb_], writes=[csm_b])
                act(csm[:, 26:28], csm[:, 24:26], AF.Exp, [csm_b], [csm_b])
                tt("dve", csm[:, 28:29], csm[:, 27:28], csm[:, 26:27], ALU.subtract, [csm_b], [csm_b])
                ts("dve", csm[:, 30:31], csm[:, 28:29], -lam_init, None, ALU.add, None, [csm_b], [csm_b])
                dma("sp", rb32[:], rel_bias[:, :], [], [tb_], tcn)
                pt, pb = ps_next(PROJ[0])
                mm(pt[0:12, 0:383], rb32[:], oht[:, 0:383], True, True, [tb_], [pb])
                copy("dve", trs[:, 0:383], pt[0:12, 0:383], [pb], [tb_])
                memset("dve", trs[:, 383:384], 0.0, [tb_])
                b_trd = P.buf()
                dma("sp", trd[:, :], trs[:], [tb_], [b_trd], tcn)
                ft_b = P.buf(multi=True)
                for k in range(128):
                    dma("sp", Ft[k:k + 1, :, :], trd[:, 127 - k:127 - k + 256].unsqueeze(0), [b_trd], [ft_b], tcn)
                for h in range(12):
                    act(Mall[:, h, :], Ft[:, h, :], AF.Exp, [ft_b, csm_b], [M_b], bias=csm[:, 12 + h:13 + h])
                    tt("pool", Mall[:, h, 0:128], Mall[:, h, 0:128], maskD, ALU.mult, [M_b, b_cb], [M_b])
                P.barrier()
                P.flush()
            alloc_attn(st, T, nm=2)
            kT = sb(st, "kT", [128, S], BF16)
            k_b = [P.buf() for _ in range(NG)]
            vt = sb(st, "vt", [128, NT, 128], BF16)
            v_b = [P.buf() for _ in range(NG)]

            def fin_factory(h, gw):
                def fin(qb, acc):
                    (n0, n0b), (n1, n1b), (d0, d0b), (d1, d1b) = acc
                    r0, r0b, _ = T["rs"].next()
                    t0, t0b, _ = T["zq"].next()
                    r1, r1b, _ = T["rs"].next()
                    t1, t1b, _ = T["zq"].next()
                    copy("dve", r0[:], d0[:], [d0b], [r0b])
                    copy("dve", r1[:], d1[:], [d1b], [r1b])
                    copy("dve", t0[:], n0[:], [n0b], [t0b])
                    copy("dve", t1[:], n1[:], [n1b], [t1b])
                    pre = GATE_PRE[0]
                    act(r0[:], r0[:], AF.Ln, [r0b], [r0b])
                    act(r0[:], r0[:], AF.Exp, [r0b], [r0b], scale=-1.0)
                    act(r1[:], r1[:], AF.Ln, [r1b], [r1b])
                    act(r1[:], r1[:], AF.Exp, [r1b], [r1b], scale=-1.0)
                    tt("dve", t0[:], t0[:], r0[:], ALU.mult, [t0b, r0b], [t0b])
                    tt("dve", t1[:], t1[:], r1[:], ALU.mult, [t1b, r1b], [t1b])
                    stt(t0[:], t1[:], csm[:, 30:31], t0[:], ALU.mult, ALU.add, [t1b, t0b, csm_b], [t0b])
                    on, onb, _ = T["on"].next()
                    rms_from_sbuf(T, t0, t0b, T["gsm"][:, 4:5], T["gsm_b"], on[:], onb, defer=True)
                    deferred.append(lambda: store_branch(T, on[:], [onb], gw, h, qb, pre=pre))
                return fin

            def mask_fn_factory(h):
                def mask_fn(qb, kt, m):
                    rel = kt - 4 * qb
                    if rel == -1:
                        return [(0, 128, Mall[:, h, 128:256], [M_b])]
                    if 0 <= rel <= 2:
                        return [(rel * 128, 256, Mall[:, h, 0:256], [M_b])]
                    if rel == 3:
                        return [(384, 128, Mall[:, h, 0:128], [M_b])]
                    return []
                return mask_fn

            nxt = {}

            def issue(hh):
                nxt[hh] = (load_w(T["w"], wsrc, hh * 128), load_w(T["w"], wsrc, MIXW + hh * 128))

            issue(0)
            for h in range(12):
                wq, wk = nxt.pop(h)
                for tg in range(NG):
                    pt, pb = proj_fm(wq[0], wq[1], T["hT"], T["h_b"], tg)
                    rms_fm_epilogue(T, pt, pb, T["gsm"][:, 2:3], T["gsm_b"],
                                    T["qT"][:, tg * 512:(tg + 1) * 512], T["q_b"][tg], half=True)
                for tg in range(NG):
                    pt, pb = proj_fm(wk[0], wk[1], T["hT"], T["h_b"], tg)
                    rms_fm_epilogue(T, pt, pb, T["gsm"][:, 3:4], T["gsm_b"],
                                    kT[:, tg * 512:(tg + 1) * 512], k_b[tg], half=True)
                wv = load_w(T["w"], wsrc, 2 * MIXW + h * 128)
                proj_v(T, wv[0], wv[1], vt, v_b)
                gw = load_w(T["w"], wsrc, gate_off + h * 128)
                attention(T, NG, T["qT"], T["q_b"],
                          kfn=lambda kt, m: (kT[m * 64:(m + 1) * 64, kt * 128:(kt + 1) * 128],
                                             slice(m * 64, (m + 1) * 64), [k_b[kt // 4]]),
                          vfn=lambda kt: (vt[:, kt, :], [v_b[kt // 4]]),
                          nkt_fn=lambda qb: 4 * qb + 4,
                          collo_fn=lambda qb, kt: max(0, kt - 4 * qb) * 128,
                          bias_fn=lambda kt: (None, []),
                          mask_fn=mask_fn_factory(h),
                          aux=None, nm=2, finish=fin_factory(h, gw), gate_w=gw,
                          mid_cb=(lambda h=h: issue(h + 1)) if h < 11 else None)

        xsrc = x_in
        chain = [xs1, xs2]
        for li, (L, kind, j) in enumerate(layers):
            xdst = y_out if li == len(layers) - 1 else chain[li % 2]
            with ExitStack() as st:
                T = common_tensors(st, kind)
                hT, hbufs = T["hT"], T["h_b"]
                with ExitStack() as st1:
                    xs, stat = p1_tensors(st1)
                    p1_norm_transpose((xs, T["gfm"], stat), xsrc, S, norm_g[L], hT, lambda t: hbufs[t // 4], bx(xsrc))
                    P.barrier()
                    P.flush()
                load_small(T, L, kind, j)
                st_mix = ExitStack()
                st.enter_context(st_mix)
                PROJ[0] = {0: [0, 1, 2, 3, 4, 5, 6, 7], 1: [7, 2, 3, 4], 2: [4, 5, 6, 7]}.get(kind, [5, 6, 7])
                if kind < 0:
                    wsrc = a_w_in[j]
                    memq_off, gate_off = 2 * MIXW, 2 * MIXW + MEMW
                    for cbi in range(12):
                        for qb in range(NG):
                            stg, stgb, stgc = T["stg"].next()
                            memset("pool", stg[:], 0.0, [stgb])
                            dma("sp", brT[cbi * 128:(cbi + 1) * 128, qb * 512:(qb + 1) * 512], stg[:], [stgb], [b_brT], stgc)
                elif kind == 0:
                    wsrc = a_w_in[j]
                    memq_off, gate_off = 2 * MIXW, 2 * MIXW + MEMW
                    layer_a(st_mix, T, L, j, wsrc, gate_off)
                elif kind == 1:
                    wsrc = b_w_in[j]
                    memq_off, gate_off = 3 * MIXW + 12, 3 * MIXW + 12 + MEMW
                    layer_b(st_mix, T, L, j, wsrc, gate_off)
                else:
                    wsrc = c_w_in[j]
                    memq_off, gate_off = 3 * MIXW, 3 * MIXW + MEMW
                    layer_c(st_mix, T, L, j, wsrc, gate_off)
                P.barrier()
                P.flush()
                st_mix.close()
                with ExitStack() as st2:
                    PROJ[0] = [7, 2, 3, 4]
                    mem_heads(st2, T, L, wsrc, memq_off, gate_off, hT, hbufs)
                    P.barrier()
                    P.flush()
            with ExitStack() as st:
                wo = sb(st, "wo", [128, NCH, D], BF16)
                wo_b = [P.buf() for _ in range(NCH)]
                wo_c = get_chans("wo", NCH)
                br = RotC(st, "br", 2, [128, NCH, 512], BF16)
                xo = RotC(st, "xo", 2, [128, D], F32)
                xn = RotC(st, "xn", 2, [128, D], F32)
                for c in range(NCH):
                    dma("pool", wo[:, c, :], w_out[L][c * 128:(c + 1) * 128, :], [], [wo_b[c]], wo_c[c])
                for tg in range(NG):
                    bt, bb, bc = br.next()
                    dma("sp", bt[:], brT[:, tg * 512:(tg + 1) * 512].rearrange("(c p) n -> p c n", p=128),
                        [b_brT], [bb], bc)
                    for t4 in range(4):
                        t = tg * 4 + t4
                        xt, xb, xc = xo.next()
                        dma("sp", xt[:], xsrc[t * 128:(t + 1) * 128, :], [bx(xsrc)], [xb], xc)
                        banks = [(psum[i], b_ps[i]) for i in ((0, 1, 2, 3) if t % 2 == 0 else (4, 5, 6, 7))]
                        for c in range(NCH):
                            for jd in range(4):
                                pt, pb = banks[jd]
                                mm(pt[:], bt[:, c, t4 * 128:(t4 + 1) * 128], wo[:, c, jd * 512:(jd + 1) * 512],
                                   c == 0, c == NCH - 1, [bb, wo_b[c]], [pb])
                        nt_, nb_, ncn = xn.next()
                        for jd in range(4):
                            pt, pb = banks[jd]
                            tt("dve", nt_[:, jd * 512:(jd + 1) * 512], pt[:], xt[:, jd * 512:(jd + 1) * 512],
                               ALU.add, [pb, xb], [nb_])
                        dma("sp", xdst[t * 128:(t + 1) * 128, :], nt_[:], [nb_], [bx(xdst)], ncn)
                P.barrier()
                P.flush()
            xsrc = xdst
    return nc


_PARAMS = ("mem_norm_g", "rel_bias", "norm_g", "w_mem_kv", "mem_q_norm_g", "mem_k_norm_g", "w_out",
           "a_w_in", "a_ln_g", "a_ln_b", "a_w_s", "a_b_s", "b_w_in", "b_b_f", "b_q_norm_g", "b_k_norm_g",
           "c_w_in", "c_q_norm_g", "c_k_norm_g", "c_lam", "c_subln_g")


def kernel(**inputs):
    x = np.asarray(inputs["x"], dtype=np.float32)
    mem = np.asarray(inputs["mem"], dtype=np.float32)
    B, S, _ = x.shape
    layers = [(i, i % 3, i // 3) for i in range(4)]
    nc = build(S, layers)
    cbc, cfc, selc, ohc = make_consts()
    params = {k: np.ascontiguousarray(np.asarray(inputs[k], dtype=np.float32)) for k in _PARAMS}
    in_maps = []
    for b in range(B):
        m = dict(params)
        m["x"] = np.ascontiguousarray(x[b])
        m["mem"] = np.ascontiguousarray(mem[b])
        m["cbf"] = cbc
        m["cf32"] = cfc
        m["selc"] = selc
        m["ohc"] = ohc
        in_maps.append(m)
    res = run_bass_kernel_spmd(nc, in_maps, core_ids=list(range(B)))
    return np.stack([np.asarray(r["y"], dtype=np.float32) for r in res.results], axis=0)
```

```python
import math
from contextlib import ExitStack

import ml_dtypes
import numpy as np

import concourse.bass as bass
import concourse.mybir as mybir
from concourse.bass_utils import run_bass_kernel_spmd

F32 = mybir.dt.float32
BF16 = mybir.dt.bfloat16
AF = mybir.ActivationFunctionType
ALU = mybir.AluOpType

D = 2048
NCH = 16
NMEM = 256
MIXW = 1536
MEMW = 512
EPS = 1e-6
ENGS = ("sp", "pe", "act", "dve", "pool")
SAME_RAW = True


class Buf:
    def __init__(self, multi=False, excl=False):
        self.w = {}
        self.r = {}
        self.multi = multi
        self.excl = excl


class Chan:
    def __init__(self, sem):
        self.sem = sem
        self.total = 0


class Prog:
    def __init__(self, nc, es):
        self.nc = nc
        self.es = es
        self.recs = {e: [] for e in ENGS}
        self.cnt = {e: 0 for e in ENGS}
        self.seen = {e: {} for e in ENGS}
        self.sem = {e: es.enter_context(nc.semaphore("s_" + e)) for e in ENGS}
        self.chans = []
        self.bufs = []
        self.nchan = 0
        self.pre_barrier = None

    def buf(self, multi=False, excl=False):
        b = Buf(multi, excl)
        self.bufs.append(b)
        return b

    def chan(self):
        self.nchan += 1
        c = Chan(self.es.enter_context(self.nc.semaphore("c%d" % self.nchan)))
        self.chans.append(c)
        return c

    def _semh(self, key):
        return key.sem if isinstance(key, Chan) else self.sem[key]

    def _need(self, eng, waits, ev, kind, is_dma=False):
        key, val, src = ev
        if src == eng and not is_dma:
            if eng == "pe" or not SAME_RAW:
                return
        if self.seen[eng].get(key, 0) >= val:
            return
        self.seen[eng][key] = val
        waits.append((key, val))

    def op(self, eng, fn, reads=(), writes=(), chan=None):
        waits = []
        isd = chan is not None
        for b in reads:
            for ev in b.w.values():
                self._need(eng, waits, ev, "raw", isd)
            if b.excl:
                for ev in b.r.values():
                    if ev[2] != eng:
                        self._need(eng, waits, ev, "rar", isd)
        for b in writes:
            if not b.multi:
                for ev in b.w.values():
                    self._need(eng, waits, ev, "waw", isd)
            for ev in b.r.values():
                self._need(eng, waits, ev, "war", isd)
        if chan is None:
            self.cnt[eng] += 1
            ev = (eng, self.cnt[eng], eng)
            inc = (eng, 1)
        else:
            chan.total += 16
            ev = (chan, chan.total, "dma")
            inc = (chan, 16)
        self.recs[eng].append((waits, fn, inc))
        for b in reads:
            b.r[ev[0]] = ev
        for b in writes:
            if b.multi:
                b.w[ev[0]] = ev
            else:
                b.w = {ev[0]: ev}
                b.r = {}

    def barrier(self):
        if self.pre_barrier is not None:
            self.pre_barrier()
        for e in ENGS:
            waits = []
            for e2 in ENGS:
                if e2 != e and self.cnt[e2] > self.seen[e].get(e2, 0):
                    self.seen[e][e2] = self.cnt[e2]
                    waits.append((e2, self.cnt[e2]))
            for c in self.chans:
                if c.total > self.seen[e].get(c, 0):
                    self.seen[e][c] = c.total
                    waits.append((c, c.total))
            if waits:
                self.recs[e].append((waits, None, None))
        for b in self.bufs:
            b.w = {}
            b.r = {}

    def flush(self):
        with self.nc.Block() as blk:
            for eng, deco in (("sp", blk.sync), ("pe", blk.tensor), ("act", blk.scalar),
                              ("dve", blk.vector), ("pool", blk.gpsimd)):
                recs = self.recs[eng]
                if not recs:
                    continue

                def body(e, recs=recs):
                    for waits, fn, inc in recs:
                        for key, val in waits:
                            e.wait_ge(self._semh(key), val)
                        if fn is None:
                            continue
                        ins = fn(e)
                        ins.then_inc(self._semh(inc[0]), inc[1])

                deco(body)
                self.recs[eng] = []


def _t5_bucket(rel):
    nb = 16
    max_exact = 8
    ret = np.where(rel > 0, nb, 0)
    n = np.abs(rel)
    nf = np.maximum(n, 1).astype(np.float32)
    large = max_exact + (np.log(nf / max_exact) / math.log(128 / max_exact) * (nb - max_exact)).astype(np.int32)
    large = np.minimum(large, nb - 1)
    return ret + np.where(n < max_exact, n, large)


CB_ID, CB_ONES, CB_BONES, CB_MF, CB_MD = 0, 128, 256, 384, 512
NCB = 640
CF_ID, CF_MW, CF_ONES = 0, 128, 256
NCF = 384


def make_consts():
    cb = np.zeros((128, NCB), np.float32)
    i = np.arange(128)
    cb[:, CB_ID:CB_ID + 128] = np.eye(128)
    cb[:, CB_ONES:CB_ONES + 128] = 1.0
    cb[:, CB_BONES:CB_BONES + 128] = (i[:, None] // 64 == i[None, :] // 64)
    cb[:, CB_MF:CB_MF + 128] = (i[:, None] <= i[None, :])
    cb[:, CB_MD:CB_MD + 128] = (i[:, None] // 64 <= i[None, :] // 64)
    sel = np.zeros((128, 12 * 128), np.float32)
    for h in range(12):
        for g in range(3):
            sel[32 * g + h, 128 * h:128 * (h + 1)] = 1.0
    cf = np.zeros((128, NCF), np.float32)
    cf[:, CF_ID:CF_ID + 128] = np.eye(128)
    cf[:, CF_MW:CF_MW + 128] = (i[:, None] // 64 >= i[None, :] // 64)
    cf[:, CF_ONES:CF_ONES + 128] = 1.0
    j = np.arange(383)
    bk = _t5_bucket(127 - j)
    oh = np.zeros((32, 384), np.float32)
    for b in range(32):
        oh[b, 0:383] = (bk == b)
    return cb.astype(ml_dtypes.bfloat16), cf, sel.astype(ml_dtypes.bfloat16), oh


class K:
    pass


def build(S, layers, debug_out=None):
    NT = S // 128
    NG = S // 512
    nc = bass.Bass("TRN2", target_bir_lowering=False)
    dt = nc.dram_tensor

    def din(name, shape, dtype=F32):
        return dt(name, list(shape), dtype, kind="ExternalInput").ap()

    x_in = din("x", [S, D])
    mem_in = din("mem", [NMEM, D])
    mem_norm_g = din("mem_norm_g", [D])
    rel_bias = din("rel_bias", [32, 12])
    norm_g = din("norm_g", [4, D])
    w_mem_kv = din("w_mem_kv", [4, D, 1024])
    mem_q_norm_g = din("mem_q_norm_g", [4, 128])
    mem_k_norm_g = din("mem_k_norm_g", [4, 128])
    w_out = din("w_out", [4, D, D])
    a_w_in = din("a_w_in", [2, D, 5632])
    a_ln_g = din("a_ln_g", [2, MIXW])
    a_ln_b = din("a_ln_b", [2, MIXW])
    a_w_s = din("a_w_s", [2, 12, 128, 128])
    a_b_s = din("a_b_s", [2, 12, 128])
    b_w_in = din("b_w_in", [1, D, 7180])
    b_b_f = din("b_b_f", [1, 12])
    b_q_norm_g = din("b_q_norm_g", [1, 128])
    b_k_norm_g = din("b_k_norm_g", [1, 128])
    c_w_in = din("c_w_in", [1, D, 7168])
    c_q_norm_g = din("c_q_norm_g", [1, 64])
    c_k_norm_g = din("c_k_norm_g", [1, 64])
    c_lam = din("c_lam", [1, 4, 64])
    c_subln_g = din("c_subln_g", [1, 128])
    cbf_in = din("cbf", [128, NCB], BF16)
    cf_in = din("cf32", [128, NCF])
    sel_in = din("selc", [128, 12 * 128], BF16)
    oh_in = din("ohc", [32, 384])
    y_out = dt("y", [S, D], F32, kind="ExternalOutput").ap()
    xs1 = dt("xscr1", [S, D], F32, kind="Internal").ap()
    xs2 = dt("xscr2", [S, D], F32, kind="Internal").ap()
    brT = dt("brT", [D, S], BF16, kind="Internal").ap()
    svT = dt("svT", [MIXW, S], BF16, kind="Internal").ap()
    trd = dt("trd", [12, 384], F32, kind="Internal").ap()

    es = ExitStack()
    with es:
        P = Prog(nc, es)

        uid = [0]

        def sb(stack, name, shape, dtype):
            uid[0] += 1
            return stack.enter_context(nc.sbuf_tensor("%s_%d" % (name, uid[0]), list(shape), dtype))

        cb = sb(es, "cb", [128, NCB], BF16)
        cf = sb(es, "cf", [128, NCF], F32)
        b_cb, b_cf, b_memT = P.buf(), P.buf(), P.buf()
        memTd = dt("memTd", [128, NCH * NMEM], BF16, kind="Internal").ap()
        b_memTd = P.buf(multi=True)
        psbig = [es.enter_context(nc.psum_tensor("ps%d" % i, [128, 1024], F32)) for i in range(4)]
        psum = [psbig[i // 2][:, (i % 2) * 512:(i % 2 + 1) * 512] for i in range(8)]
        b_ps = [P.buf(excl=True) for _ in range(8)]
        c_const = P.chan()
        ident = cb[:, CB_ID:CB_ID + 128]
        ones = cb[:, CB_ONES:CB_ONES + 128]
        bones = cb[:, CB_BONES:CB_BONES + 128]
        maskF = cb[:, CB_MF:CB_MF + 128]
        maskD = cb[:, CB_MD:CB_MD + 128]
        ident32 = cf[:, CF_ID:CF_ID + 128]
        maskW = cf[:, CF_MW:CF_MW + 128]
        ones32 = cf[:, CF_ONES:CF_ONES + 128]
        b_brT = P.buf(multi=True)
        b_svT = P.buf(multi=True)
        b_x = {id(t): P.buf(multi=True) for t in (x_in, xs1, xs2, y_out)}
        bx = lambda t: b_x[id(t)]

        P.op("sp", lambda e: e.dma_start(out=cb[:], in_=cbf_in[:, :]), writes=[b_cb], chan=c_const)
        P.op("sp", lambda e: e.dma_start(out=cf[:], in_=cf_in[:, :]), writes=[b_cf], chan=c_const)

        def mm(out, lhsT, rhs, start, stop, reads, writes):
            P.op("pe", lambda e: e.matmul(out, lhsT, rhs, start=start, stop=stop), reads=reads, writes=writes)

        def act(out, in_, func, reads, writes, bias=None, scale=None):
            kw = {}
            if bias is not None:
                kw["bias"] = bias
            if scale is not None:
                kw["scale"] = scale
            P.op("act", lambda e: e.activation(out=out, in_=in_, func=func, **kw), reads=reads, writes=writes)

        def tt(eng, out, in0, in1, op, reads, writes):
            P.op(eng, lambda e: e.tensor_tensor(out=out, in0=in0, in1=in1, op=op), reads=reads, writes=writes)

        def ts(eng, out, in0, s1, s2, op0, op1, reads, writes):
            if op1 is None:
                P.op(eng, lambda e: e.tensor_scalar(out=out, in0=in0, scalar1=s1, scalar2=None, op0=op0),
                     reads=reads, writes=writes)
            else:
                P.op(eng, lambda e: e.tensor_scalar(out=out, in0=in0, scalar1=s1, scalar2=s2, op0=op0, op1=op1),
                     reads=reads, writes=writes)

        def stt(out, in0, scalar, in1, op0, op1, reads, writes):
            P.op("dve", lambda e: e.scalar_tensor_tensor(out=out, in0=in0, scalar=scalar, in1=in1, op0=op0, op1=op1),
                 reads=reads, writes=writes)

        def recip(out, in_, reads, writes):
            P.op("dve", lambda e: e.reciprocal(out=out, in_=in_), reads=reads, writes=writes)

        deferred = []

        def run_deferred():
            while deferred:
                deferred.pop(0)()

        PROJ = [[5, 6, 7]]
        P.pre_barrier = run_deferred

        def copy(eng, out, in_, reads, writes):
            if eng == "act":
                P.op("act", lambda e: e.copy(out=out, in_=in_), reads=reads, writes=writes)
            else:
                P.op(eng, lambda e: e.tensor_copy(out=out, in_=in_), reads=reads, writes=writes)

        def dma(eng, out, in_, reads, writes, chan, **kw):
            P.op(eng, lambda e: e.dma_start(out=out, in_=in_, **kw), reads=reads, writes=writes, chan=chan)

        def memset(eng, ap, val, writes):
            P.op(eng, lambda e: e.memset(ap, val), writes=writes)

        class Rot:
            def __init__(self, stack, name, n, shape, dtype, chans=False):
                self.t = [sb(stack, "%s%d" % (name, i), shape, dtype) for i in range(n)]
                self.b = [P.buf() for _ in range(n)]
                self.c = [P.chan() for _ in range(n)] if chans else [None] * n
                self.i = 0
                self.n = n

            def next(self):
                i = self.i
                self.i = (i + 1) % self.n
                return self.t[i], self.b[i], self.c[i]

        ps_rr = [0]

        def ps_next(cands):
            i = cands[ps_rr[0] % len(cands)]
            ps_rr[0] += 1
            return psum[i], b_ps[i]

        chan_pool = {}

        def get_chans(name, n):
            if name not in chan_pool:
                chan_pool[name] = [P.chan() for _ in range(n)]
            return chan_pool[name]

        class RotC(Rot):
            def __init__(self, stack, name, n, shape, dtype):
                self.t = [sb(stack, "%s%d" % (name, i), shape, dtype) for i in range(n)]
                self.b = [P.buf() for _ in range(n)]
                self.c = get_chans(name, n)
                self.i = 0
                self.n = n

        def p1_norm_transpose(stack_tensors, src, ntok, gsrc_ap, dstT, dst_bufs_for_tile, src_buf):
            xs, gfm, stat = stack_tensors
            gt, gb, gc = gfm
            dma("sp", gt[:], gsrc_ap.rearrange("(c p) -> p c", p=128), [], [gb], gc, allow_slow_non_contiguous=True)
            for t in range(ntok // 128):
                xt, xb, xc = xs.next()
                st, stb, _ = stat.next()
                dma("sp", xt[:], src[t * 128:(t + 1) * 128, :], [src_buf], [xb], xc)
                jt, jb, _ = xs.junk
                P.op("act", lambda e, jt=jt, xt=xt, st=st: e.activation(out=jt[:], in_=xt[:], func=AF.Square,
                                                                         accum_out=st[:, 0:1]),
                     reads=[xb], writes=[jb, stb])
                act(st[:, 1:2], st[:, 0:1], AF.Ln, [stb, b_eps], [stb], bias=EPS_AP[0], scale=1.0 / D)
                act(st[:, 2:3], st[:, 1:2], AF.Exp, [stb], [stb], scale=-0.5)
                ts("dve", xt[:], xt[:], st[:, 2:3], None, ALU.mult, None, [xb, stb], [xb])
                for jb4 in range(NCH // 4):
                    pt, pb = ps_next([4, 5, 6, 7])
                    for c4 in range(4):
                        c = jb4 * 4 + c4
                        P.op("pe", lambda e, pt=pt, c4=c4, c=c, xt=xt: e.transpose(
                            pt[:, c4 * 128:(c4 + 1) * 128], xt[:, c * 128:(c + 1) * 128], ident32),
                            reads=[xb, b_cf], writes=[pb])
                    tt("dve", dstT[:, jb4 * 4:(jb4 + 1) * 4, t * 128:(t + 1) * 128],
                       pt[:, :].rearrange("p (c n) -> p c n", c=4),
                       gt[:, jb4 * 4:(jb4 + 1) * 4].unsqueeze(2).broadcast_to([128, 4, 128]),
                       ALU.mult, [pb, gb], [dst_bufs_for_tile(t)])

        epst = sb(es, "epst", [128, 4], F32)
        b_eps = P.buf()
        memset("dve", epst[:, 0:1], EPS, [b_eps])
        memset("dve", epst[:, 1:2], 0.0, [b_eps])
        memset("dve", epst[:, 2:3], 1.0, [b_eps])
        EPS_AP = [epst[:, 0:1], epst[:, 1:2], epst[:, 2:3]]

        def load_w(rot, wsrc, col0, ncols=128, cols_out=None):
            wt, wb, wc = rot.next()
            o = wt[:, :, 0:ncols] if cols_out is None else cols_out(wt)
            dma("pool", o, wsrc[:, col0:col0 + ncols].rearrange("(c p) n -> p c n", p=128), [], [wb], wc)
            return wt, wb

        def proj_fm(wt, wb, hT, hbufs, tg, ntok=512):
            pt, pb = ps_next(PROJ[0])
            for c in range(NCH):
                mm(pt[:, 0:ntok], wt[:, c, :], hT[:, c, tg * 512:tg * 512 + ntok], c == 0, c == NCH - 1,
                   [wb, hbufs[tg]], [pb])
            return pt, pb

        def rms_fm_epilogue(tmp, pt, pb, gcol, gbuf, out_ap, out_buf, ntok=512, half=False):
            zq, zb, _ = tmp["zq"].next()
            copy("dve", zq[:, 0:ntok], pt[:, 0:ntok], [pb], [zb])
            rms_from_sbuf(tmp, zq, zb, gcol, gbuf, out_ap, out_buf, ntok, half, defer=True)

        def rms_from_sbuf(tmp, zq, zb, gcol, gbuf, out_ap, out_buf, ntok=512, half=False, defer=False):
            sq, sqb, _ = tmp["sq"].next()
            rs, rsb, _ = tmp["rs"].next()
            tt("pool", sq[:, 0:ntok], zq[:, 0:ntok], zq[:, 0:ntok], ALU.mult, [zb], [sqb])

            def part2():
                p2, p2b = ps_next(PROJ[0])
                mm(p2[:, 0:ntok], bones if half else ones, sq[:, 0:ntok], True, True, [sqb, b_cb], [p2b])
                act(rs[:, 0:ntok], p2[:, 0:ntok], AF.Ln, [p2b, b_eps], [rsb], bias=EPS_AP[0],
                    scale=1.0 / (64 if half else 128))
                act(rs[:, 0:ntok], rs[:, 0:ntok], AF.Exp, [rsb], [rsb], scale=-0.5)
                stt(out_ap, zq[:, 0:ntok], gcol, rs[:, 0:ntok], ALU.mult, ALU.mult, [zb, rsb, gbuf], [out_buf])

            if defer:
                run_deferred()
                deferred.append(part2)
            else:
                part2()

        GATE_PRE = [None]

        def attention(T, nqb, qT, qbufs, kfn, vfn, nkt_fn, collo_fn, bias_fn, mask_fn, aux, nm,
                      finish, gate_w=None, mid_cb=None):
            run_deferred()
            if nm == 1:
                accs = [[(psum[0], b_ps[0]), (psum[1], b_ps[1])], [(psum[5], b_ps[5]), (psum[6], b_ps[6])]]
                sbanks = [[(psum[2], b_ps[2])], [(psum[3], b_ps[3])], [(psum[4], b_ps[4])]]
            else:
                accs = [[(psum[0], b_ps[0]), (psum[1], b_ps[1]), (psum[2], b_ps[2]), (psum[3], b_ps[3])]]
                sbanks = [[(psum[4], b_ps[4]), (psum[5], b_ps[5])], [(psum[6], b_ps[6]), (psum[7], b_ps[7])]]
            LOOK = len(sbanks) - 1
            for qb in range(nqb):
                acc = accs[qb % len(accs)]
                q0 = qb * 512
                pairs = list(range(nkt_fn(qb)))
                npairs = len(pairs)
                sstate = {}

                def emit_s(i):
                    kt = pairs[i]
                    lo = collo_fn(qb, kt)
                    sb_ = sbanks[i % len(sbanks)]
                    for m in range(nm):
                        st_, stb_ = sb_[m]
                        lhsT, psl, kb = kfn(kt, m)
                        mm(st_[:, lo:512], lhsT, qT[psl, q0 + lo:q0 + 512], True, aux is None,
                           kb + [qbufs[qb]], [stb_])
                        if aux is not None:
                            sel_ap, rows, rbuf = aux
                            mm(st_[:, lo:512], sel_ap, rows[:, q0 + lo:q0 + 512], False, True,
                               [b_cb, rbuf], [stb_])
                    sstate[i] = (kt, lo, sb_)

                def emit_pv(i):
                    kt, lo, sb_ = sstate.pop(i)
                    if nm == 2:
                        pT2, pT2b, _ = T["pT2"].next()
                        big = psbig[2 + (i % 2)]
                        act(pT2[:, :, lo:512], big[:, :].rearrange("p (m n) -> p m n", m=2)[:, :, lo:512], AF.Exp,
                            [sb_[0][1], sb_[1][1]], [pT2b])
                    for m in range(nm):
                        st_, stb_ = sb_[m]
                        if nm == 2:
                            pT, pTb = pT2[:, m, :], pT2b
                        else:
                            pT, pTb, _ = T["pT"].next()
                        bias_ap, bias_bufs = bias_fn(kt)
                        if nm == 2:
                            pass
                        elif bias_ap is None:
                            act(pT[:, lo:512], st_[:, lo:512], AF.Exp, [stb_], [pTb])
                        else:
                            act(pT[:, lo:512], st_[:, lo:512], AF.Exp, [stb_] + bias_bufs, [pTb], bias=bias_ap)
                        if nm == 1:
                            for (c0, mk_ap, mk_bufs) in mask_fn(qb, kt, m):
                                tt("pool", pT[:, c0:c0 + 128], pT[:, c0:c0 + 128], mk_ap, ALU.mult,
                                   [pTb] + mk_bufs, [pTb])
                        elif m == 0:
                            for (c0, w_, mk_ap, mk_bufs) in mask_fn(qb, kt, m):
                                tt("dve", pT2[:, :, c0:c0 + w_], pT2[:, :, c0:c0 + w_],
                                   mk_ap.unsqueeze(1).broadcast_to([128, 2, w_]), ALU.mult,
                                   [pT2b] + mk_bufs, [pT2b])
                        vl, vb = vfn(kt)
                        num, numb = acc[m]
                        den, denb = acc[nm + m]
                        mm(num[:, lo:512], vl, pT[:, lo:512], i == 0, i == npairs - 1, vb + [pTb], [numb])
                        mm(den[:, lo:512], ones, pT[:, lo:512], i == 0, i == npairs - 1, [b_cb, pTb], [denb])

                for i in range(min(LOOK, npairs)):
                    emit_s(i)
                for i in range(npairs):
                    if i + LOOK < npairs:
                        emit_s(i + LOOK)
                    emit_pv(i)
                    if i == min(8, npairs - 1):
                        save = PROJ[0]
                        PROJ[0] = [[2, 3, 4][i % 3]] if nm == 1 else [4 + 2 * (i % 2)]
                        run_deferred()
                        if gate_w is not None:
                            PROJ[0] = [7] if nm == 1 else [5 + 2 * (i % 2)]
                            GATE_PRE[0] = gate_proj(T, gate_w, qb)
                        PROJ[0] = save
                        if qb == 0 and mid_cb is not None:
                            mid_cb()
                finish(qb, acc)

        def gate_proj(T, gw, qb):
            wt, wb = gw
            pt, pb = proj_fm(wt, wb, T["hT"], T["h_b"], qb)
            gt, gtb, _ = T["g"].next()
            e_, eb, _ = T["rs"].next()
            xs_, xsb, _ = T["gx"].next()
            copy("act", xs_[:], pt[:], [pb], [xsb])
            act(e_[:], xs_[:], AF.Exp, [xsb], [eb], scale=-1.0)
            act(e_[:], e_[:], AF.Ln, [eb, b_eps], [eb], bias=EPS_AP[2])
            act(e_[:], e_[:], AF.Exp, [eb], [eb], scale=-1.0)
            tt("pool", gt[:], xs_[:], e_[:], ALU.mult, [xsb, eb], [gtb])
            return gt, gtb

        def store_branch(T, o_ap, o_bufs, gw, cb_idx, qb, pre=None):
            gt, gtb = gate_proj(T, gw, qb) if pre is None else pre
            stg, stgb, stgc = T["stg"].next()
            tt("pool", stg[:], o_ap, gt[:], ALU.mult, o_bufs + [gtb], [stgb])
            dma("sp", brT[cb_idx * 128:(cb_idx + 1) * 128, qb * 512:(qb + 1) * 512], stg[:], [stgb], [b_brT], stgc)

        def finish_simple(T, gw, cb_idx):
            def fin(qb, acc):
                (num, numb), (den, denb) = acc
                pre = GATE_PRE[0]
                rd, rdb, _ = T["rs"].next()
                o, ob, _ = T["zq"].next()
                recip(rd[:], den[:], [denb], [rdb])
                tt("dve", o[:], num[:], rd[:], ALU.mult, [numb, rdb], [ob])
                store_branch(T, o[:], [ob], gw, cb_idx, qb, pre=pre)
            return fin

        def mem_kv(T, L):
            mkT, mv, memT = T["mkT"], T["mv"], T["memT"]
            gk = T["gsm"]
            for hm in range(4):
                wt, wb = load_w(T["w"], w_mem_kv[L], hm * 128)
                pt, pb = ps_next(PROJ[0])
                for c in range(NCH):
                    mm(pt[:, 0:NMEM], wt[:, c, :], memT[:, c, :], c == 0, c == NCH - 1, [wb, b_memT], [pb])
                rms_fm_epilogue(T, pt, pb, gk[:, 0:1], T["gsm_b"], mkT[:, hm, :], T["mkT_b"], ntok=NMEM)
            for hm in range(4):
                wt, wb = load_w(T["w"], w_mem_kv[L], MEMW + hm * 128)
                pt, pb = ps_next(PROJ[0])
                for mt in range(2):
                    for c in range(NCH):
                        mm(pt[:, mt * 128:(mt + 1) * 128], memT[:, c, mt * 128:(mt + 1) * 128], wt[:, c, :],
                           c == 0, c == NCH - 1, [wb, b_memT], [pb])
                copy("act", mv[:, :, hm * 128:(hm + 1) * 128], pt[:, 0:256].rearrange("p (t n) -> p t n", t=2),
                     [pb], [T["mv_b"]])

        def mem_heads(st, T, L, wsrc, memq_off, gate_off, hT, hbufs):
            alloc_attn(st, T)
            T["memT"] = sb(st, "memT", [128, NCH, NMEM], BF16)
            T["mkT"] = sb(st, "mkT", [128, 4, NMEM], BF16)
            T["mkT_b"] = P.buf()
            T["mv"] = sb(st, "mv", [128, 2, 512], BF16)
            T["mv_b"] = P.buf()
            dma("sp", T["memT"][:].rearrange("p c n -> p (c n)"), memTd[:, :], [b_memTd], [b_memT], get_chans("memT", 1)[0])
            mem_kv(T, L)
            nxtm = {0: load_w(T["w"], wsrc, memq_off)}

            def issue_m(hh):
                nxtm[hh] = load_w(T["w"], wsrc, memq_off + hh * 128)

            for hm in range(4):
                wt, wb = nxtm.pop(hm)
                for tg in range(NG):
                    pt, pb = proj_fm(wt, wb, hT, hbufs, tg)
                    rms_fm_epilogue(T, pt, pb, T["gsm"][:, 1:2], T["gsm_b"],
                                    T["qT"][:, tg * 512:(tg + 1) * 512], T["q_b"][tg])
                gw = load_w(T["w"], wsrc, gate_off + MIXW + hm * 128)
                attention(T, NG, T["qT"], T["q_b"],
                          kfn=lambda kt, m, hm=hm: (T["mkT"][:, hm, kt * 128:(kt + 1) * 128], slice(0, 128), [T["mkT_b"]]),
                          vfn=lambda kt, hm=hm: (T["mv"][:, kt, hm * 128:(hm + 1) * 128], [T["mv_b"]]),
                          nkt_fn=lambda qb: 2, collo_fn=lambda qb, kt: 0,
                          bias_fn=lambda kt: (None, []), mask_fn=lambda qb, kt, m: [],
                          aux=None, nm=1, finish=finish_simple(T, gw, 12 + hm), gate_w=gw,
                          mid_cb=(lambda hm=hm: issue_m(hm + 1)) if hm < 3 else None)

        def load_small(T, L, kind, j):
            g = T["gsm"]
            gb = T["gsm_b"]
            gc = T["gsm_c"]
            tmp = T["gsm_raw"]
            dma("sp", tmp[:, 0:1], mem_k_norm_g[L].unsqueeze(1), [], [gb], gc)
            dma("sp", tmp[:, 1:2], mem_q_norm_g[L].unsqueeze(1), [], [gb], gc)
            if kind == 1:
                dma("sp", tmp[:, 2:3], b_q_norm_g[j].unsqueeze(1), [], [gb], gc)
                dma("sp", tmp[:, 3:4], b_k_norm_g[j].unsqueeze(1), [], [gb], gc)
            if kind == 2:
                for hh in range(2):
                    dma("sp", tmp[hh * 64:(hh + 1) * 64, 2:3], c_q_norm_g[j].unsqueeze(1), [], [gb], gc)
                    dma("sp", tmp[hh * 64:(hh + 1) * 64, 3:4], c_k_norm_g[j].unsqueeze(1), [], [gb], gc)
                dma("sp", tmp[:, 4:5], c_subln_g[j].unsqueeze(1), [], [gb], gc)
            copy("dve", g[:, 0:1], tmp[:, 0:1], [gb], [gb])
            ts("dve", g[:, 1:2], tmp[:, 1:2], 128 ** -0.5, None, ALU.mult, None, [gb], [gb])
            if kind == 1:
                ts("dve", g[:, 2:3], tmp[:, 2:3], 128 ** -0.5, None, ALU.mult, None, [gb], [gb])
                copy("dve", g[:, 3:4], tmp[:, 3:4], [gb], [gb])
            if kind == 2:
                ts("dve", g[:, 2:3], tmp[:, 2:3], 64 ** -0.5, None, ALU.mult, None, [gb], [gb])
                copy("dve", g[:, 3:4], tmp[:, 3:4], [gb], [gb])
                lam_init = 0.8 - 0.6 * math.exp(-0.3 * L)
                ts("dve", g[:, 4:5], tmp[:, 4:5], 1.0 - lam_init, None, ALU.mult, None, [gb], [gb])

        def alloc_attn(stk, T, nm=1):
            T["qT"] = sb(stk, "qT", [128, S], BF16)
            T["q_b"] = [P.buf() for _ in range(NG)]
            if nm == 1:
                T["pT"] = Rot(stk, "pT", 4, [128, 512], BF16)
            else:
                T["pT2"] = Rot(stk, "pT2", 3, [128, 2, 512], BF16)
                T["on"] = Rot(stk, "on", 2, [128, 512], F32)

        def common_tensors(st, kind):
            T = {}
            T["hT"] = sb(st, "hT", [128, NCH, S], BF16)
            T["h_b"] = [P.buf() for _ in range(NG)]
            T["w"] = RotC(st, "w", 3, [128, NCH, 128], BF16)
            T["g"] = Rot(st, "g", 3 if kind == 2 else 2, [128, 512], BF16)
            T["gx"] = Rot(st, "gx", 2, [128, 512], BF16)
            T["zq"] = Rot(st, "zq", 3 if kind == 0 else 4, [128, 512], F32)
            T["sq"] = Rot(st, "sq", 2 if kind == 0 else 3, [128, 512], BF16)
            T["rs"] = Rot(st, "rs", 2 if kind == 0 else 3, [128, 512], F32)
            T["stg"] = RotC(st, "stg", 2, [128, 512], BF16)
            T["gsm"] = sb(st, "gsm", [128, 8], F32)
            T["gsm_raw"] = sb(st, "gsmr", [128, 8], F32)
            T["gsm_b"] = P.buf()
            T["gsm_c"] = get_chans("gsm", 1)[0]
            T["gfm"] = (sb(st, "gfm", [128, NCH], F32), P.buf(), get_chans("gfm", 1)[0])
            return T

        def p1_tensors(st):
            xs = RotC(st, "xs", 2, [128, D], F32)
            xs.junk = (sb(st, "xjunk", [128, D], BF16), P.buf(), None)
            stat = Rot(st, "stat", 2, [128, 4], F32)
            return xs, stat

        with ExitStack() as st:
            xs, stat = p1_tensors(st)
            gfm = (sb(st, "gfm0", [128, NCH], F32), P.buf(), get_chans("gfm", 1)[0])
            b_memsrc = P.buf()
            memT0 = sb(st, "memT0", [128, NCH, NMEM], BF16)
            p1_norm_transpose((xs, gfm, stat), mem_in, NMEM, mem_norm_g, memT0, lambda t: b_memT, b_memsrc)
            dma("sp", memTd[:, :], memT0[:].rearrange("p c n -> p (c n)"), [b_memT], [b_memTd], get_chans("memT", 1)[0])
            P.barrier()
            P.flush()

        def proj_v(T, wt, wb, vt, vbufs):
            for tb in range(NG):
                pt, pb = ps_next(PROJ[0])
                for t4 in range(4):
                    t = tb * 4 + t4
                    for c in range(NCH):
                        mm(pt[:, t4 * 128:(t4 + 1) * 128], T["hT"][:, c, t * 128:(t + 1) * 128], wt[:, c, :],
                           c == 0, c == NCH - 1, [wb, T["h_b"][tb]], [pb])
                copy("act", vt[:, tb * 4:(tb + 1) * 4, :], pt[:].rearrange("p (t n) -> p t n", t=4), [pb], [vbufs[tb]])

        def layer_b(st, T, L, j, wsrc, gate_off):
            alloc_attn(st, T)
            kT = sb(st, "kT", [128, S], BF16)
            k_b = [P.buf() for _ in range(NG)]
            vt = sb(st, "vt", [128, NT, 128], BF16)
            v_b = [P.buf() for _ in range(NG)]
            rows = sb(st, "rows", [128, S], BF16)
            rows_b = P.buf()
            selt = sb(st, "selt", [128, 12 * 128], BF16)
            dma("sp", selt[:], sel_in[:, :], [], [b_cb], get_chans("selt", 1)[0])
            negc = sb(st, "negc", [128, NT, 12], F32)
            negc_b = P.buf()
            sm = sb(st, "fsm", [128, 4], F32)
            sm_b = P.buf()
            smc = get_chans("fsm", 1)[0]
            memset("dve", sm[:], 0.0, [sm_b])
            memset("dve", sm[:, 2:3], 1.0, [sm_b])
            for g in range(3):
                dma("sp", sm[32 * g:32 * g + 12, 0:1], b_b_f[j].unsqueeze(1), [], [sm_b], smc)
            wt, wb, wc = T["w"].next()
            P.op("pool", lambda e: e.memset(wt[:], 0.0), writes=[wb])
            for g in range(3):
                dma("pool", wt[:, :, 32 * g:32 * g + 12],
                    wsrc[:, 3 * MIXW:3 * MIXW + 12].rearrange("(c p) n -> p c n", p=128), [], [wb], wc)
            for tg in range(NG):
                pt, pb = proj_fm(wt, wb, T["hT"], T["h_b"], tg)
                ls, lsb, _ = T["zq"].next()
                cc, ccb, _ = T["rs"].next()
                act(ls[:], pt[:], AF.Sigmoid, [pb, sm_b], [lsb], bias=sm[:, 0:1])
                act(ls[:], ls[:], AF.Ln, [lsb], [lsb])
                init = 0.0 if tg == 0 else sm[:, 1:2]
                P.op("dve", lambda e, cc=cc, ls=ls, init=init: e.tensor_tensor_scan(
                    out=cc[:], data0=sm[:, 2:3].broadcast_to([128, 512]), data1=ls[:], initial=init,
                    op0=ALU.mult, op1=ALU.add), reads=[lsb, sm_b], writes=[ccb])
                copy("dve", sm[:, 1:2], cc[:, 511:512], [ccb], [sm_b])
                sl = slice(tg * 512, (tg + 1) * 512)
                copy("act", rows[:, sl], cc[:], [ccb], [rows_b])
                r1, r1b, _ = T["zq"].next()
                tt("dve", r1[:], cc[:], rows[:, sl], ALU.subtract, [ccb, rows_b], [r1b])
                md, mdb, _ = T["sq"].next()
                copy("act", md[:], r1[:], [r1b], [mdb])
                copy("pool", rows[32:64, sl], md[32:64, :], [mdb], [rows_b])
                tt("dve", r1[:], r1[:], md[:], ALU.subtract, [r1b, mdb], [r1b])
                copy("act", rows[64:96, sl], r1[64:96, :], [r1b], [rows_b])
                p2, p2b = ps_next(PROJ[0])
                for i4 in range(4):
                    P.op("pe", lambda e, p2=p2, cc=cc, i4=i4: e.transpose(
                        p2[:, i4 * 32:(i4 + 1) * 32], cc[0:32, i4 * 128:(i4 + 1) * 128], ident32[0:32, 0:32]),
                        reads=[ccb, b_cf], writes=[p2b])
                ts("dve", negc[:, tg * 4:(tg + 1) * 4, :],
                   p2[:, 0:128].rearrange("p (t n) -> p t n", t=4)[:, :, 0:12], -1.0, None, ALU.mult, None,
                   [p2b], [negc_b])
            nxt = {}

            def issue(hh):
                nxt[hh] = (load_w(T["w"], wsrc, hh * 128), load_w(T["w"], wsrc, MIXW + hh * 128))

            issue(0)
            for h in range(12):
                wq, wk = nxt.pop(h)
                for tg in range(NG):
                    pt, pb = proj_fm(wq[0], wq[1], T["hT"], T["h_b"], tg)
                    rms_fm_epilogue(T, pt, pb, T["gsm"][:, 2:3], T["gsm_b"],
                                    T["qT"][:, tg * 512:(tg + 1) * 512], T["q_b"][tg])
                for tg in range(NG):
                    pt, pb = proj_fm(wk[0], wk[1], T["hT"], T["h_b"], tg)
                    rms_fm_epilogue(T, pt, pb, T["gsm"][:, 3:4], T["gsm_b"],
                                    kT[:, tg * 512:(tg + 1) * 512], k_b[tg])
                wv = load_w(T["w"], wsrc, 2 * MIXW + h * 128)
                proj_v(T, wv[0], wv[1], vt, v_b)
                gw = load_w(T["w"], wsrc, gate_off + h * 128)
                sel_h = selt[:, 128 * h:128 * (h + 1)]
                attention(T, NG, T["qT"], T["q_b"],
                          kfn=lambda kt, m: (kT[:, kt * 128:(kt + 1) * 128], slice(0, 128), [k_b[kt // 4]]),
                          vfn=lambda kt: (vt[:, kt, :], [v_b[kt // 4]]),
                          nkt_fn=lambda qb: 4 * qb + 4,
                          collo_fn=lambda qb, kt: max(0, kt - 4 * qb) * 128,
                          bias_fn=lambda kt, h=h: (negc[:, kt, h:h + 1], [negc_b]),
                          mask_fn=lambda qb, kt, m: ([((kt - 4 * qb) * 128, maskF, [b_cb])] if kt >= 4 * qb else []),
                          aux=(sel_h, rows, rows_b), nm=1, finish=finish_simple(T, gw, h), gate_w=gw,
                          mid_cb=(lambda h=h: issue(h + 1)) if h < 11 else None)

        def layer_a(st, T, L, j, wsrc, gate_off):
            hT, hbufs = T["hT"], T["h_b"]
            addT = sb(st, "addT", [128, 12, 128], F32)
            wmT = sb(st, "wmT", [128, 12, 128], BF16)
            gb12 = sb(st, "gb12", [128, 24], F32)
            bc_b, wm_b = P.buf(), P.buf()
            bcc = get_chans("abc", 1)[0]
            dma("sp", gb12[:, 0:12], a_ln_g[j].rearrange("(g c) -> c g", c=128), [], [bc_b], bcc,
                allow_slow_non_contiguous=True)
            dma("sp", gb12[:, 12:24], a_ln_b[j].rearrange("(g c) -> c g", c=128), [], [bc_b], bcc,
                allow_slow_non_contiguous=True)
            with ExitStack() as st_t:
                bsbc = sb(st_t, "bsbc", [128, 12, 128], F32)
                wmf = Rot(st_t, "wmf", 2, [128, 128], F32)
                wl = RotC(st_t, "wsl", 2, [128, 128], F32)
                bs_b = P.buf()
                dma("sp", bsbc[:].rearrange("p g t -> p (g t)"),
                    a_b_s[j].rearrange("g t -> (g t)").partition_broadcast(128), [], [bs_b], get_chans("abs", 1)[0])
                for g in range(12):
                    wt_, wtb, wtc = wl.next()
                    dma("sp", wt_[:], a_w_s[j, g], [], [wtb], wtc)
                    tt("dve", wt_[:], wt_[:], maskW, ALU.mult, [wtb, b_cf], [wtb])
                    pt, pb = ps_next(PROJ[0])
                    P.op("pe", lambda e, pt=pt, wt_=wt_: e.transpose(pt[:, 0:128], wt_[:], ident32),
                         reads=[wtb, b_cf], writes=[pb])
                    copy("act", wmT[:, g, :], pt[:, 0:128], [pb], [wm_b])
                    p2, p2b = ps_next(PROJ[0])
                    mm(p2[:, 0:128], ones, wmT[:, g, :], True, True, [wm_b, b_cb], [p2b])
                    stt(addT[:, g, :], p2[:, 0:128], gb12[:, 12 + g:13 + g], bsbc[:, g, :], ALU.mult, ALU.add,
                        [p2b, bc_b, bs_b], [wm_b])
                P.barrier()
                P.flush()
            gv = [sb(st, "gv", [128, 4, MIXW], BF16) for _ in range(2)]
            gv_b = [[P.buf() for _ in range(4)] for _ in range(2)]
            vh = Rot(st, "vh", 3, [128, MIXW], BF16)
            lnst = Rot(st, "lnst", 4, [128, 24], F32)
            svs = RotC(st, "svs", 2, [128, 4, 128], BF16)
            svl = RotC(st, "svl", 2, [128, 512], BF16)

            def v_group(gi):
                gvt, gvb = gv[gi % 2], gv_b[gi % 2]
                for vb in range(12):
                    wt, wb = load_w(T["w"], wsrc, MIXW + vb * 128)
                    pt, pb = ps_next(PROJ[0])
                    for t4 in range(4):
                        t = gi * 4 + t4
                        for c in range(NCH):
                            mm(pt[:, t4 * 128:(t4 + 1) * 128], hT[:, c, t * 128:(t + 1) * 128], wt[:, c, :],
                               c == 0, c == NCH - 1, [wb, hbufs[gi]], [pb])
                    act(gvt[:, :, vb * 128:(vb + 1) * 128], pt[:].rearrange("p (t n) -> p t n", t=4),
                        AF.Gelu_apprx_tanh, [pb], gvb)

            def ln_group(gi):
                gvt, gvb = gv[gi % 2], gv_b[gi % 2]
                vts = []

                def ln_tile(t4):
                        ls_, lsb, _ = lnst.next()
                        for k3 in range(3):
                            P.op("dve", lambda e, ls_=ls_, k3=k3, t4=t4, gvt=gvt: e.bn_stats(
                                out=ls_[:, k3 * 6:(k3 + 1) * 6], in_=gvt[:, t4, k3 * 512:(k3 + 1) * 512]),
                                reads=[gvb[t4]], writes=[lsb])
                        P.op("dve", lambda e, ls_=ls_: e.bn_aggr(out=ls_[:, 18:20], in_=ls_[:, 0:18]),
                             reads=[lsb], writes=[lsb])
                        act(ls_[:, 20:21], ls_[:, 19:20], AF.Ln, [lsb, b_eps], [lsb], bias=EPS_AP[0])
                        act(ls_[:, 21:22], ls_[:, 20:21], AF.Exp, [lsb], [lsb], scale=-0.5)
                        vt_, vtb, _ = vh.next()
                        ts("dve", vt_[:], gvt[:, t4, :], ls_[:, 18:19], ls_[:, 21:22], ALU.subtract, ALU.mult,
                           [gvb[t4], lsb], [vtb])
                        vts.append((vt_, vtb))

                for t4 in range(3):
                    ln_tile(t4)
                for t4 in range(4):
                    if t4 == 1:
                        ln_tile(3)
                    t = gi * 4 + t4
                    vt_, vtb = vts[t4]
                    for g4 in range(3):
                        pt, pb = ps_next(PROJ[0])
                        for gi4 in range(4):
                            g = g4 * 4 + gi4
                            mm(pt[:, gi4 * 128:(gi4 + 1) * 128], vt_[:, g * 128:(g + 1) * 128], wmT[:, g, :],
                               True, True, [vtb, wm_b], [pb])
                        tm_, tmb, _ = T["zq"].next()
                        tt("dve", tm_[:].rearrange("p (g t) -> p g t", g=4), pt[:].rearrange("p (g t) -> p g t", g=4),
                           gb12[:, g4 * 4:(g4 + 1) * 4].unsqueeze(2).broadcast_to([128, 4, 128]), ALU.mult,
                           [pb, bc_b], [tmb])
                        sv_, svb, svc = svs.next()
                        tt("pool", sv_[:], tm_[:].rearrange("p (g t) -> p g t", g=4), addT[:, g4 * 4:(g4 + 1) * 4, :],
                           ALU.add, [tmb, wm_b], [svb])
                        dma("sp", svT[g4 * 512:(g4 + 1) * 512, t * 128:(t + 1) * 128].rearrange("(g c) n -> c g n", c=128),
                            sv_[:], [svb], [b_svT], svc)

            v_group(0)
            for gi in range(1, NG):
                v_group(gi)
                ln_group(gi - 1)
            ln_group(NG - 1)
            wu_next = load_w(T["w"], wsrc, 0)
            for g in range(12):
                wu = wu_next
                gw = load_w(T["w"], wsrc, gate_off + g * 128)
                for tg in range(NG):
                    if tg == NG // 2 and g < 11:
                        wu_next = load_w(T["w"], wsrc, (g + 1) * 128)
                    sl_, slb, slc = svl.next()
                    dma("sp", sl_[:], svT[g * 128:(g + 1) * 128, tg * 512:(tg + 1) * 512], [b_svT], [slb], slc)
                    pt, pb = proj_fm(wu[0], wu[1], hT, hbufs, tg)
                    u_, ub, _ = T["zq"].next()
                    act(u_[:], pt[:], AF.Gelu_apprx_tanh, [pb], [ub])
                    tt("pool", u_[:], u_[:], sl_[:], ALU.mult, [ub, slb], [ub])
                    store_branch(T, u_[:], [ub], gw, g, tg)

        def layer_c(st, T, L, j, wsrc, gate_off):
            Mall = sb(st, "Mall", [128, 12, 256], BF16)
            M_b = P.buf()
            csm = sb(st, "csm", [128, 40], F32)
            csm_b = P.buf()
            cc_ = get_chans("csm", 1)[0]
            lam_init = 0.8 - 0.6 * math.exp(-0.3 * L)
            dma("sp", csm[:, 0:12], rel_bias[15].partition_broadcast(128), [], [csm_b], cc_)
            ts("dve", csm[:, 12:24], csm[:, 0:12], -1.0, None, ALU.mult, None, [csm_b], [csm_b])
            with ExitStack() as st_t:
                tb_ = P.buf()
                lamt = sb(st_t, "lamt", [128, 256], F32)
                Ft = sb(st_t, "Ft", [128, 12, 256], F32)
                rb32 = sb(st_t, "rb32", [32, 12], F32)
                oht = sb(st_t, "oht", [32, 384], F32)
                dma("sp", oht[:], oh_in[:, :], [], [tb_], get_chans("oht", 1)[0])
                trs = sb(st_t, "trs", [12, 384], F32)
                tcn = get_chans("ctmp", 1)[0]
                dma("sp", lamt[:], c_lam[j].rearrange("a d -> (a d)").partition_broadcast(128), [], [tb_], tcn)
                tt("dve", lamt[:, 0:64], lamt[:, 0:64], lamt[:, 64:128], ALU.mult, [tb_], [tb_])
                tt("dve", lamt[:, 128:192], lamt[:, 128:192], lamt[:, 192:256], ALU.mult, [tb_], [tb_])
                P.op("dve", lambda e: e.reduce_sum(out=csm[:, 24:25], in_=lamt[:, 0:64], axis=mybir.AxisListType.X),
                     reads=[tb_], writes=[csm_b])
                P.op("dve", lambda e: e.reduce_sum(out=csm[:, 25:26], in_=lamt[:, 128:192], axis=mybir.AxisListType.X),
                     reads=[tb_], writes=[csm_b])
                act(csm[:, 26:28], csm[:, 24:26], AF.Exp, [csm_b], [csm_b])
                tt("dve", csm[:, 28:29], csm[:, 27:28], csm[:, 26:27], ALU.subtract, [csm_b], [csm_b])
                ts("dve", csm[:, 30:31], csm[:, 28:29], -lam_init, None, ALU.add, None, [csm_b], [csm_b])
                dma("sp", rb32[:], rel_bias[:, :], [], [tb_], tcn)
                pt, pb = ps_next(PROJ[0])
                mm(pt[0:12, 0:383], rb32[:], oht[:, 0:383], True, True, [tb_], [pb])
                copy("dve", trs[:, 0:383], pt[0:12, 0:383], [pb], [tb_])
                memset("dve", trs[:, 383:384], 0.0, [tb_])
                b_trd = P.buf()
                dma("sp", trd[:, :], trs[:], [tb_], [b_trd], tcn)
                ft_b = P.buf(multi=True)
                for k in range(128):
                    dma("sp", Ft[k:k + 1, :, :], trd[:, 127 - k:127 - k + 256].unsqueeze(0), [b_trd], [ft_b], tcn)
                for h in range(12):
                    act(Mall[:, h, :], Ft[:, h, :], AF.Exp, [ft_b, csm_b], [M_b], bias=csm[:, 12 + h:13 + h])
                    tt("pool", Mall[:, h, 0:128], Mall[:, h, 0:128], maskD, ALU.mult, [M_b, b_cb], [M_b])
                P.barrier()
                P.flush()
            alloc_attn(st, T, nm=2)
            kT = sb(st, "kT", [128, S], BF16)
            k_b = [P.buf() for _ in range(NG)]
            vt = sb(st, "vt", [128, NT, 128], BF16)
            v_b = [P.buf() for _ in range(NG)]

            def fin_factory(h, gw):
                def fin(qb, acc):
                    (n0, n0b), (n1, n1b), (d0, d0b), (d1, d1b) = acc
                    r0, r0b, _ = T["rs"].next()
                    t0, t0b, _ = T["zq"].next()
                    r1, r1b, _ = T["rs"].next()
                    t1, t1b, _ = T["zq"].next()
                    copy("dve", r0[:], d0[:], [d0b], [r0b])
                    copy("act", t0[:], n0[:], [n0b], [t0b])
                    copy("dve", r1[:], d1[:], [d1b], [r1b])
                    copy("act", t1[:], n1[:], [n1b], [t1b])
                    pre = GATE_PRE[0]
                    act(r0[:], r0[:], AF.Ln, [r0b], [r0b])
                    act(r0[:], r0[:], AF.Exp, [r0b], [r0b], scale=-1.0)
                    act(r1[:], r1[:], AF.Ln, [r1b], [r1b])
                    act(r1[:], r1[:], AF.Exp, [r1b], [r1b], scale=-1.0)
                    tt("dve", t0[:], t0[:], r0[:], ALU.mult, [t0b, r0b], [t0b])
                    tt("dve", t1[:], t1[:], r1[:], ALU.mult, [t1b, r1b], [t1b])
                    stt(t0[:], t1[:], csm[:, 30:31], t0[:], ALU.mult, ALU.add, [t1b, t0b, csm_b], [t0b])
                    on, onb, _ = T["on"].next()
                    rms_from_sbuf(T, t0, t0b, T["gsm"][:, 4:5], T["gsm_b"], on[:], onb, defer=True)
                    deferred.append(lambda: store_branch(T, on[:], [onb], gw, h, qb, pre=pre))
                return fin

            def mask_fn_factory(h):
                def mask_fn(qb, kt, m):
                    rel = kt - 4 * qb
                    if rel == -1:
                        return [(0, 128, Mall[:, h, 128:256], [M_b])]
                    if 0 <= rel <= 2:
                        return [(rel * 128, 256, Mall[:, h, 0:256], [M_b])]
                    if rel == 3:
                        return [(384, 128, Mall[:, h, 0:128], [M_b])]
                    return []
                return mask_fn

            nxt = {}

            def issue(hh):
                nxt[hh] = (load_w(T["w"], wsrc, hh * 128), load_w(T["w"], wsrc, MIXW + hh * 128))

            issue(0)
            for h in range(12):
                wq, wk = nxt.pop(h)
                for tg in range(NG):
                    pt, pb = proj_fm(wq[0], wq[1], T["hT"], T["h_b"], tg)
                    rms_fm_epilogue(T, pt, pb, T["gsm"][:, 2:3], T["gsm_b"],
                                    T["qT"][:, tg * 512:(tg + 1) * 512], T["q_b"][tg], half=True)
                for tg in range(NG):
                    pt, pb = proj_fm(wk[0], wk[1], T["hT"], T["h_b"], tg)
                    rms_fm_epilogue(T, pt, pb, T["gsm"][:, 3:4], T["gsm_b"],
                                    kT[:, tg * 512:(tg + 1) * 512], k_b[tg], half=True)
                wv = load_w(T["w"], wsrc, 2 * MIXW + h * 128)
                proj_v(T, wv[0], wv[1], vt, v_b)
                gw = load_w(T["w"], wsrc, gate_off + h * 128)
                attention(T, NG, T["qT"], T["q_b"],
                          kfn=lambda kt, m: (kT[m * 64:(m + 1) * 64, kt * 128:(kt + 1) * 128],
                                             slice(m * 64, (m + 1) * 64), [k_b[kt // 4]]),
                          vfn=lambda kt: (vt[:, kt, :], [v_b[kt // 4]]),
                          nkt_fn=lambda qb: 4 * qb + 4,
                          collo_fn=lambda qb, kt: max(0, kt - 4 * qb) * 128,
                          bias_fn=lambda kt: (None, []),
                          mask_fn=mask_fn_factory(h),
                          aux=None, nm=2, finish=fin_factory(h, gw), gate_w=gw,
                          mid_cb=(lambda h=h: issue(h + 1)) if h < 11 else None)

        xsrc = x_in
        chain = [xs1, xs2]
        for li, (L, kind, j) in enumerate(layers):
            xdst = y_out if li == len(layers) - 1 else chain[li % 2]
            with ExitStack() as st:
                T = common_tensors(st, kind)
                hT, hbufs = T["hT"], T["h_b"]
                with ExitStack() as st1:
                    xs, stat = p1_tensors(st1)
                    p1_norm_transpose((xs, T["gfm"], stat), xsrc, S, norm_g[L], hT, lambda t: hbufs[t // 4], bx(xsrc))
                    P.barrier()
                    P.flush()
                load_small(T, L, kind, j)
                st_mix = ExitStack()
                st.enter_context(st_mix)
                PROJ[0] = {0: [0, 1, 2, 3, 4, 5, 6, 7], 1: [7, 2, 3, 4], 2: [4, 5, 6, 7]}.get(kind, [5, 6, 7])
                if kind < 0:
                    wsrc = a_w_in[j]
                    memq_off, gate_off = 2 * MIXW, 2 * MIXW + MEMW
                    for cbi in range(12):
                        for qb in range(NG):
                            stg, stgb, stgc = T["stg"].next()
                            memset("pool", stg[:], 0.0, [stgb])
                            dma("sp", brT[cbi * 128:(cbi + 1) * 128, qb * 512:(qb + 1) * 512], stg[:], [stgb], [b_brT], stgc)
                elif kind == 0:
                    wsrc = a_w_in[j]
                    memq_off, gate_off = 2 * MIXW, 2 * MIXW + MEMW
                    layer_a(st_mix, T, L, j, wsrc, gate_off)
                elif kind == 1:
                    wsrc = b_w_in[j]
                    memq_off, gate_off = 3 * MIXW + 12, 3 * MIXW + 12 + MEMW
                    layer_b(st_mix, T, L, j, wsrc, gate_off)
                else:
                    wsrc = c_w_in[j]
                    memq_off, gate_off = 3 * MIXW, 3 * MIXW + MEMW
                    layer_c(st_mix, T, L, j, wsrc, gate_off)
                P.barrier()
                P.flush()
                st_mix.close()
                with ExitStack() as st2:
                    PROJ[0] = [7, 2, 3, 4]
                    mem_heads(st2, T, L, wsrc, memq_off, gate_off, hT, hbufs)
                    P.barrier()
                    P.flush()
            with ExitStack() as st:
                wo = sb(st, "wo", [128, NCH, D], BF16)
                wo_b = [P.buf() for _ in range(NCH)]
                wo_c = get_chans("wo", NCH)
                br = RotC(st, "br", 2, [128, NCH, 512], BF16)
                xo = RotC(st, "xo", 2, [128, D], F32)
                xn = RotC(st, "xn", 2, [128, D], F32)
                for c in range(NCH):
                    dma("pool", wo[:, c, :], w_out[L][c * 128:(c + 1) * 128, :], [], [wo_b[c]], wo_c[c])
                for tg in range(NG):
                    bt, bb, bc = br.next()
                    dma("sp", bt[:], brT[:, tg * 512:(tg + 1) * 512].rearrange("(c p) n -> p c n", p=128),
                        [b_brT], [bb], bc)
                    for t4 in range(4):
                        t = tg * 4 + t4
                        xt, xb, xc = xo.next()
                        dma("sp", xt[:], xsrc[t * 128:(t + 1) * 128, :], [bx(xsrc)], [xb], xc)
                        banks = [(psum[i], b_ps[i]) for i in ((0, 1, 2, 3) if t % 2 == 0 else (4, 5, 6, 7))]
                        for c in range(NCH):
                            for jd in range(4):
                                pt, pb = banks[jd]
                                mm(pt[:], bt[:, c, t4 * 128:(t4 + 1) * 128], wo[:, c, jd * 512:(jd + 1) * 512],
                                   c == 0, c == NCH - 1, [bb, wo_b[c]], [pb])
                        nt_, nb_, ncn = xn.next()
                        for jd in range(4):
                            pt, pb = banks[jd]
                            tt("dve", nt_[:, jd * 512:(jd + 1) * 512], pt[:], xt[:, jd * 512:(jd + 1) * 512],
                               ALU.add, [pb, xb], [nb_])
                        dma("sp", xdst[t * 128:(t + 1) * 128, :], nt_[:], [nb_], [bx(xdst)], ncn)
                P.barrier()
                P.flush()
            xsrc = xdst
    return nc


_PARAMS = ("mem_norm_g", "rel_bias", "norm_g", "w_mem_kv", "mem_q_norm_g", "mem_k_norm_g", "w_out",
           "a_w_in", "a_ln_g", "a_ln_b", "a_w_s", "a_b_s", "b_w_in", "b_b_f", "b_q_norm_g", "b_k_norm_g",
           "c_w_in", "c_q_norm_g", "c_k_norm_g", "c_lam", "c_subln_g")


def kernel(**inputs):
    x = np.asarray(inputs["x"], dtype=np.float32)
    mem = np.asarray(inputs["mem"], dtype=np.float32)
    B, S, _ = x.shape
    layers = [(i, i % 3, i // 3) for i in range(4)]
    nc = build(S, layers)
    cbc, cfc, selc, ohc = make_consts()
    params = {k: np.ascontiguousarray(np.asarray(inputs[k], dtype=np.float32)) for k in _PARAMS}
    in_maps = []
    for b in range(B):
        m = dict(params)
        m["x"] = np.ascontiguousarray(x[b])
        m["mem"] = np.ascontiguousarray(mem[b])
        m["cbf"] = cbc
        m["cf32"] = cfc
        m["selc"] = selc
        m["ohc"] = ohc
        in_maps.append(m)
    res = run_bass_kernel_spmd(nc, in_maps, core_ids=list(range(B)))
    return np.stack([np.asarray(r["y"], dtype=np.float32) for r in res.results], axis=0)
```
